# Optimizing a Trainium2 kernel written in Bass

```python
import jax, jax.numpy as jnp
from jax import lax
import numpy as np

D_MODEL = 1024
BATCH = 8
SEQ = 2048
DEPTH = 4

CTX_LEN = 256
GRID_W = 64
N_MIXERS = 2
N_HEADS = 16
N_KV_HEADS = 4
HEAD_DIM = D_MODEL // N_HEADS
KV_GROUP = N_HEADS // N_KV_HEADS
ROPE_PAIRS_PER_AXIS = HEAD_DIM // 4
ROPE_THETA = 10000.0
Q_BLOCK = 128
CONV_WIDTH = 31
CONV_PAD = CONV_WIDTH // 2
D_FF = (((8 * D_MODEL + 2) // 3 + 255) // 256) * 256
N_CONV_LAYERS = (DEPTH + N_MIXERS - 1) // N_MIXERS
N_ATTN_LAYERS = DEPTH // N_MIXERS
N_MOD = 6
EPS = 1e-6

kernel_name = "hybrid_conv_gqa_diffusion_trunk"


def rmsnorm(x, g):
    xf = x.astype(jnp.float32)
    y = xf * lax.rsqrt(jnp.mean(xf * xf, axis=-1, keepdims=True) + EPS)
    return (y * g.astype(jnp.float32)).astype(x.dtype)


def layernorm(x, g, b):
    xf = x.astype(jnp.float32)
    mu = jnp.mean(xf, axis=-1, keepdims=True)
    var = jnp.mean(jnp.square(xf - mu), axis=-1, keepdims=True)
    y = (xf - mu) * lax.rsqrt(var + EPS)
    return (y * g.astype(jnp.float32) + b.astype(jnp.float32)).astype(x.dtype)


def axial_rope_tables(row, col):
    freqs = ROPE_THETA ** (-jnp.arange(ROPE_PAIRS_PER_AXIS, dtype=jnp.float32) / ROPE_PAIRS_PER_AXIS)
    ang = jnp.concatenate([row.astype(jnp.float32)[:, None] * freqs[None, :],
                           col.astype(jnp.float32)[:, None] * freqs[None, :]], axis=-1)
    return jnp.cos(ang), jnp.sin(ang)


def apply_rope(x, cos, sin):
    B, S, H, Dh = x.shape
    xr = x.astype(jnp.float32).reshape(B, S, H, Dh // 2, 2)
    x0, x1 = xr[..., 0], xr[..., 1]
    c = cos[None, :, None, :]
    s = sin[None, :, None, :]
    out = jnp.stack([x0 * c - x1 * s, x0 * s + x1 * c], axis=-1)
    return out.reshape(B, S, H, Dh).astype(x.dtype)


def conv_module(h, w_pw1, b_pw1, w_dw, b_dw, ln_g, ln_b, w_pw2, b_pw2):
    u = h @ w_pw1 + b_pw1
    a, gt = jnp.split(u, 2, axis=-1)
    u = a * jax.nn.sigmoid(gt)
    u = lax.conv_general_dilated(
        u, w_dw[:, None, :].astype(u.dtype), window_strides=(1,),
        padding=[(CONV_PAD, CONV_PAD)], dimension_numbers=("NWC", "WIO", "NWC"),
        feature_group_count=u.shape[-1]) + b_dw
    u = jax.nn.silu(layernorm(u, ln_g, ln_b))
    return u @ w_pw2 + b_pw2


def gqa_sdpa(q, k, v):
    B, Q, H, Dh = q.shape
    qg = q.reshape(B, Q, N_KV_HEADS, KV_GROUP, Dh)
    s = jnp.einsum("bqkgd,bskd->bkgqs", qg, k).astype(jnp.float32) * (Dh ** -0.5)
    p = jax.nn.softmax(s, axis=-1).astype(v.dtype)
    o = jnp.einsum("bkgqs,bskd->bqkgd", p, v)
    return o.reshape(B, Q, H * Dh)


def attention_module(hx, hc, wq, wk, wv, wo, q_g, k_g, cos, sin, need_ctx):
    B, S, _ = hx.shape
    L = hc.shape[1]
    qx = apply_rope(rmsnorm((hx @ wq).reshape(B, S, N_HEADS, HEAD_DIM), q_g), cos, sin)
    kx = apply_rope(rmsnorm((hx @ wk).reshape(B, S, N_KV_HEADS, HEAD_DIM), k_g), cos, sin)
    vx = (hx @ wv).reshape(B, S, N_KV_HEADS, HEAD_DIM)
    kc = rmsnorm((hc @ wk).reshape(B, L, N_KV_HEADS, HEAD_DIM), k_g)
    vc = (hc @ wv).reshape(B, L, N_KV_HEADS, HEAD_DIM)
    k_all = jnp.concatenate([kx, kc], axis=1)
    v_all = jnp.concatenate([vx, vc], axis=1)
    n_blk = S // Q_BLOCK
    qb = qx.reshape(B, n_blk, Q_BLOCK, N_HEADS, HEAD_DIM).transpose(1, 0, 2, 3, 4)
    ob = lax.map(lambda q: gqa_sdpa(q, k_all, v_all), qb)
    yx = ob.transpose(1, 0, 2, 3).reshape(B, S, N_HEADS * HEAD_DIM) @ wo
    yc = None
    if need_ctx:
        qc = rmsnorm((hc @ wq).reshape(B, L, N_HEADS, HEAD_DIM), q_g)
        yc = gqa_sdpa(qc, kc, vc) @ wo
    return yx, yc


def swiglu(h, w1, w3, w2):
    return (jax.nn.silu(h @ w1) * (h @ w3)) @ w2


def setup_inputs(seed: int = 0) -> dict:
    key = jax.random.key(seed)
    ks = jax.random.split(key, 32)
    D = D_MODEL

    def nrm(k, shape, scale):
        return jax.random.normal(k, shape, jnp.float32) * scale

    NC, NA = N_CONV_LAYERS, N_ATTN_LAYERS
    return {
        "x": nrm(ks[0], (BATCH, SEQ, D), 1.0),
        "c": nrm(ks[1], (BATCH, D), 1.0),
        "ctx": nrm(ks[2], (BATCH, CTX_LEN, D), 1.0),
        "c_ctx": nrm(ks[3], (D,), 1.0),
        "w_mod": nrm(ks[4], (DEPTH, D, N_MOD * D), 0.5 * D ** -0.5),
        "b_mod": nrm(ks[5], (DEPTH, N_MOD * D), 0.01),
        "norm_g": 1.0 + nrm(ks[6], (DEPTH, 4, D), 0.05),
        "conv_w_pw1": nrm(ks[7], (NC, D, 2 * D), D ** -0.5),
        "conv_b_pw1": nrm(ks[8], (NC, 2 * D), 0.01),
        "conv_w_dw": nrm(ks[9], (NC, CONV_WIDTH, D), CONV_WIDTH ** -0.5),
        "conv_b_dw": nrm(ks[10], (NC, D), 0.01),
        "conv_ln_g": 1.0 + nrm(ks[11], (NC, D), 0.05),
        "conv_ln_b": nrm(ks[12], (NC, D), 0.01),
        "conv_w_pw2": nrm(ks[13], (NC, D, D), D ** -0.5),
        "conv_b_pw2": nrm(ks[14], (NC, D), 0.01),
        "attn_wq": nrm(ks[15], (NA, D, N_HEADS * HEAD_DIM), D ** -0.5),
        "attn_wk": nrm(ks[16], (NA, D, N_KV_HEADS * HEAD_DIM), D ** -0.5),
        "attn_wv": nrm(ks[17], (NA, D, N_KV_HEADS * HEAD_DIM), D ** -0.5),
        "attn_wo": nrm(ks[18], (NA, N_HEADS * HEAD_DIM, D), (N_HEADS * HEAD_DIM) ** -0.5),
        "attn_q_g": 1.0 + nrm(ks[19], (NA, HEAD_DIM), 0.05),
        "attn_k_g": 1.0 + nrm(ks[20], (NA, HEAD_DIM), 0.05),
        "ffn_w1": nrm(ks[21], (DEPTH, D, D_FF), D ** -0.5),
        "ffn_w3": nrm(ks[22], (DEPTH, D, D_FF), D ** -0.5),
        "ffn_w2": nrm(ks[23], (DEPTH, D_FF, D), D_FF ** -0.5),
    }


def reference(x, c, ctx, c_ctx, w_mod, b_mod, norm_g,
              conv_w_pw1, conv_b_pw1, conv_w_dw, conv_b_dw, conv_ln_g, conv_ln_b, conv_w_pw2, conv_b_pw2,
              attn_wq, attn_wk, attn_wv, attn_wo, attn_q_g, attn_k_g,
              ffn_w1, ffn_w3, ffn_w2):
    B, S, D = x.shape
    rows = S // GRID_W
    row = jnp.repeat(jnp.arange(rows), GRID_W)
    col = jnp.tile(jnp.arange(GRID_W), rows)
    cos, sin = axial_rope_tables(row, col)
    silu_c = jax.nn.silu(c)
    silu_cc = jax.nn.silu(c_ctx)

    for i in range(DEPTH):
        need_ctx = i < DEPTH - 1
        mod_x = (silu_c @ w_mod[i] + b_mod[i])[:, None, :]
        mod_c = silu_cc @ w_mod[i] + b_mod[i]
        sh_x, sc_x, g_x, shf_x, scf_x, gf_x = jnp.split(mod_x, N_MOD, axis=-1)
        sh_c, sc_c, g_c, shf_c, scf_c, gf_c = jnp.split(mod_c, N_MOD, axis=-1)

        hx = rmsnorm(x, norm_g[i, 0]) * (1.0 + sc_x) + sh_x
        hc = rmsnorm(ctx, norm_g[i, 0]) * (1.0 + sc_c) + sh_c
        if i % N_MIXERS == 0:
            j = i // N_MIXERS
            cp = (conv_w_pw1[j], conv_b_pw1[j], conv_w_dw[j], conv_b_dw[j],
                  conv_ln_g[j], conv_ln_b[j], conv_w_pw2[j], conv_b_pw2[j])
            yx = conv_module(hx, *cp)
            yc = conv_module(hc, *cp) if need_ctx else None
        else:
            j = i // N_MIXERS
            yx, yc = attention_module(hx, hc, attn_wq[j], attn_wk[j], attn_wv[j], attn_wo[j],
                                      attn_q_g[j], attn_k_g[j], cos, sin, need_ctx)
        x = x + g_x * rmsnorm(yx, norm_g[i, 1])
        if need_ctx:
            ctx = ctx + g_c * rmsnorm(yc, norm_g[i, 1])

        fx = rmsnorm(x, norm_g[i, 2]) * (1.0 + scf_x) + shf_x
        x = x + gf_x * rmsnorm(swiglu(fx, ffn_w1[i], ffn_w3[i], ffn_w2[i]), norm_g[i, 3])
        if need_ctx:
            fc = rmsnorm(ctx, norm_g[i, 2]) * (1.0 + scf_c) + shf_c
            ctx = ctx + gf_c * rmsnorm(swiglu(fc, ffn_w1[i], ffn_w3[i], ffn_w2[i]), norm_g[i, 3])
    return x
```

```python
import numpy as np
import concourse.bass as bass
import concourse.mybir as mybir
from concourse.bass_utils import run_bass_kernel_spmd

F32 = mybir.dt.float32
BF16 = mybir.dt.bfloat16
U8 = mybir.dt.uint8
AF = mybir.ActivationFunctionType
ALU = mybir.AluOpType

_DTS = {F32: 4, BF16: 2, U8: 1}


def dtsize(dt):
    return _DTS[dt]


class _Op:
    __slots__ = ("eng", "fn", "slot", "deps", "signal", "tick", "idx")


class _Slot:
    __slots__ = ("sem", "count", "pending")


class Prog:
    ENGS = ("pe", "act", "dve", "pool", "sp")

    def __init__(self, nc):
        self.nc = nc
        self.ops = []
        self.recs = {}
        self.slots = []

    def slot(self):
        s = _Slot()
        s.sem = None
        s.count = 0
        s.pending = []
        self.slots.append(s)
        return s

    @staticmethod
    def _region(ap):
        t = ap.tensor
        tn = type(t).__name__
        if tn.startswith("DRam"):
            return None
        a = ap.ap
        es = _DTS[ap.dtype]
        pstride, pn = a[0]
        off = ap.offset
        p0 = off // pstride
        f0 = off - p0 * pstride
        ext = 1
        for s, c in a[1:]:
            ext += (c - 1) * abs(s)
        return (t.name, f0 * es, (f0 + ext) * es, p0, p0 + pn)

    def add(self, eng, fn, reads=(), writes=(), slot=None):
        op = _Op()
        op.eng = eng
        op.fn = fn
        op.slot = slot
        op.signal = False
        op.tick = 0
        op.idx = len(self.ops)
        deps = {}
        rregs = [r for r in (self._region(a) for a in reads) if r is not None]
        wregs = [r for r in (self._region(a) for a in writes) if r is not None]
        for (name, lo, hi, plo, phi) in rregs:
            for r in self.recs.get(name, ()):
                if r[5] and r[0] < hi and lo < r[1] and r[2] < phi and plo < r[3]:
                    deps[r[4]] = True
        for (name, lo, hi, plo, phi) in wregs:
            for r in self.recs.get(name, ()):
                if r[0] < hi and lo < r[1] and r[2] < phi and plo < r[3]:
                    if r[4] not in deps:
                        deps[r[4]] = False
        deps.pop(op.idx, None)
        op.deps = deps
        self.ops.append(op)
        for (name, lo, hi, plo, phi) in wregs:
            lst = self.recs.setdefault(name, [])
            lst[:] = [r for r in lst if not (lo <= r[0] and r[1] <= hi and plo <= r[2] and r[3] <= phi)]
            lst.append([lo, hi, plo, phi, op.idx, True])
        for (name, lo, hi, plo, phi) in rregs:
            lst = self.recs.setdefault(name, [])
            if slot is None:
                lst[:] = [r for r in lst if not ((not r[5]) and self.ops[r[4]].eng == eng and self.ops[r[4]].slot is None
                                                  and lo <= r[0] and r[1] <= hi and plo <= r[2] and r[3] <= phi)]
            lst.append([lo, hi, plo, phi, op.idx, False])
        return op

    def dma(self, eng, out, in_, slot):
        op = self.add(eng, lambda e: e.dma_start(out=out, in_=in_), reads=[in_], writes=[out], slot=slot)
        slot.count += 16
        slot.pending.append(op)
        return op

    def seal(self, slot):
        for op in slot.pending:
            op.tick = slot.count
        slot.pending = []

    def emit(self, final_slots=()):
        nc = self.nc
        ops = self.ops
        for s in self.slots:
            assert not s.pending, "unsealed DMA slot"
        for op in ops:
            for d, raw in op.deps.items():
                p = ops[d]
                if p.slot is not None:
                    continue
                if p.eng == op.eng and op.slot is None:
                    if p.eng == "pe" or not raw:
                        continue
                p.signal = True
        cnt = {e: 0 for e in self.ENGS}
        for op in ops:
            if op.slot is None and op.signal:
                cnt[op.eng] += 1
                op.tick = cnt[op.eng]
        import contextlib
        with contextlib.ExitStack() as st:
            esem = {e: st.enter_context(nc.semaphore("S_" + e)) for e in self.ENGS}
            for i, s in enumerate(self.slots):
                s.sem = st.enter_context(nc.semaphore("D%d" % i))
            block = st.enter_context(nc.Block())
            self.n_waits = 0

            def stream(engname, e):
                waited = {}
                for op in ops:
                    if op.eng != engname:
                        continue
                    need = {}
                    for d, raw in op.deps.items():
                        p = ops[d]
                        if p.slot is not None:
                            key = ("d", id(p.slot))
                            sem = p.slot.sem
                        else:
                            if p.eng == op.eng and op.slot is None:
                                if p.eng == "pe" or not raw:
                                    continue
                            key = ("e", p.eng)
                            sem = esem[p.eng]
                        if p.tick > need.get(key, (None, 0))[1]:
                            need[key] = (sem, p.tick)
                    for key, (sem, val) in need.items():
                        if val > waited.get(key, 0):
                            e.wait_ge(sem, val)
                            waited[key] = val
                            self.n_waits += 1
                    ins = op.fn(e)
                    if op.slot is not None:
                        ins.then_inc(op.slot.sem, 16)
                    elif op.signal:
                        ins.then_inc(esem[op.eng], 1)
                if engname == "sp":
                    for s in final_slots:
                        e.wait_ge(s.sem, s.count)

            @block.tensor
            def _(e):
                stream("pe", e)

            @block.scalar
            def _(e):
                stream("act", e)

            @block.vector
            def _(e):
                stream("dve", e)

            @block.gpsimd
            def _(e):
                stream("pool", e)

            @block.sync
            def _(e):
                stream("sp", e)


D = 1024
KC = 8
SEQ = 2048
CTXL = 256
T = SEQ + CTXL
DFF = 2816
MFF = 22
DEPTH = 4
EPS = 1e-6
NCORES = 8

V_BMOD = 0
V_NORMG = V_BMOD + 192
V_CONV = V_NORMG + 128
CONV_SZ = 296
V_ATTN = V_CONV + 2 * CONV_SZ
V_CT = V_ATTN + 4
NV = V_CT + 16

XT_OFF = 0
XT_SZ = KC * T * 4
R13_OFF = XT_OFF + XT_SZ
R13_SLOT = 2048
R13_N = 8
R2_OFF = R13_OFF + R13_SLOT * R13_N
R2_SLOT = 5632
R2_N = 2
CONST_OFF = R2_OFF + R2_SLOT * R2_N
CONST_SZ = 7168
AR_OFF = CONST_OFF + CONST_SZ
TOTAL = 210944
SCR_SZ = 24576
SCR_OFF = TOTAL - SCR_SZ
PH_SZ = SCR_OFF - AR_OFF


class _Rot:
    def __init__(self, items):
        self.items = items
        self.i = 0

    def next(self):
        v = self.items[self.i % len(self.items)]
        self.i += 1
        return v


def build_program(n_layers=DEPTH, dbg_stage=None, full_out=False):
    nc = bass.Bass("TRN2", target_bir_lowering=False)

    def dram(name, shape, kind="ExternalInput"):
        return nc.dram_tensor(name, shape, F32, kind=kind).ap()

    xT_in = dram("xT", [D, T])
    vecs_in = dram("vecs", [128, NV])
    mats_in = dram("mats", [128, 4 * 128])
    rope_in = dram("rope", [128, 2, SEQ])
    wmod_in = dram("wmod", [DEPTH * 48, 128, 1024])
    pw1_in = dram("pw1", [2 * 16, 128, 1024])
    pw2_in = dram("pw2", [2 * 8, 128, 1024])
    wq_in = dram("wq", [2 * 8, 128, 1024])
    wk_in = dram("wk", [2 * 2, 128, 1024])
    wv_in = dram("wv", [2, 128, 2048])
    wo_in = dram("wo", [2 * 8, 128, 1024])
    w1_in = dram("w1", [DEPTH * MFF, 128, 1024])
    w3_in = dram("w3", [DEPTH * MFF, 128, 1024])
    w2_in = dram("w2", [DEPTH * 8, 128, DFF])
    OUTC = T if full_out else SEQ
    yT_out = dram("yT", [D, OUTC], kind="ExternalOutput")

    P = Prog(nc)
    import contextlib
    st = contextlib.ExitStack()
    with st:
        ar = st.enter_context(nc.sbuf_tensor("ar", [128, TOTAL], U8))
        banks = [st.enter_context(nc.psum_tensor("pb%d" % i, [128, 512], F32)) for i in range(8)]
        psA = _Rot(banks[0:6])
        psB = _Rot(banks[6:8])

        def view(off, shape, dt):
            n = int(np.prod(shape))
            assert off % 4 == 0 and off + n * dtsize(dt) <= TOTAL, (off, shape)
            v = ar[:, off:off + n * dtsize(dt)].bitcast(dt)
            if len(shape) == 2:
                v = v.rearrange("p (a b) -> p a b", b=shape[1])
            elif len(shape) == 3:
                v = v.rearrange("p (a b c) -> p a b c", b=shape[1], c=shape[2])
            return v

        xT = view(XT_OFF, [KC, T], F32)
        r13 = [view(R13_OFF + i * R13_SLOT, [KC, 128], BF16) for i in range(R13_N)]
        r13_slots = [P.slot() for _ in range(R13_N)]
        r13_i = [0]
        r2_slots = [P.slot() for _ in range(R2_N)]
        r2_i = [0]
        co = [CONST_OFF]

        def calloc(shape, dt):
            n = int(np.prod(shape)) * dtsize(dt)
            n = (n + 3) // 4 * 4
            v = view(co[0], shape, dt)
            co[0] += n
            assert co[0] <= CONST_OFF + CONST_SZ
            return v

        vecs = calloc([NV], F32)
        mats = calloc([4, 128], BF16)
        ident, ones_m, blk64, swapm = mats[:, 0, :], mats[:, 1, :], mats[:, 2, :], mats[:, 3, :]
        scT = calloc([KC, 2], BF16)
        mod_sb = calloc([48, 2], F32)
        A_mix = calloc([KC, 2], F32)
        G_mix = calloc([KC, 2], F32)
        A_ffn = calloc([KC, 2], F32)
        G_ffn = calloc([KC, 2], F32)
        epsT = calloc([1], F32)

        xsq = view(SCR_OFF, [KC, 512], BF16)
        rstd_r = _Rot([view(SCR_OFF + 8192 + i * 2048, [512], F32) for i in range(2)])
        nt_r = _Rot([view(SCR_OFF + 12288 + i * 2048, [512], F32) for i in range(3)])
        ysq_r = _Rot([view(SCR_OFF + 18432 + i * 1024, [512], BF16) for i in range(2)])
        stmp_r = _Rot([view(SCR_OFF + 20480 + i * 2048, [512], F32) for i in range(2)])

        def mm(out, lhsT, rhs, start, stop):
            P.add('pe', lambda e: e.matmul(out, lhsT, rhs, start=start, stop=stop), reads=[lhsT, rhs], writes=[out])

        def act(out, in_, func, bias=None, scale=None):
            kw = {}
            reads = [in_]
            if bias is not None:
                kw['bias'] = bias
                if not isinstance(bias, float):
                    reads.append(bias)
            if scale is not None:
                kw['scale'] = scale
                if not isinstance(scale, float):
                    reads.append(scale)
            P.add('act', lambda e: e.activation(out, in_, func, **kw), reads=reads, writes=[out])

        def tt(out, in0, in1, op, eng='dve'):
            P.add(eng, lambda e: e.tensor_tensor(out, in0, in1, op), reads=[in0, in1], writes=[out])

        def stt(out, in0, scalar, in1, op0, op1, eng='dve'):
            reads = [in0, in1] + ([] if isinstance(scalar, float) else [scalar])
            P.add(eng, lambda e: e.scalar_tensor_tensor(out, in0, scalar, in1, op0, op1), reads=reads, writes=[out])

        def tsc(out, in0, s1, op0, s2=None, op1=None, eng='dve'):
            reads = [in0] + [s for s in (s1, s2) if s is not None and not isinstance(s, float)]
            if op1 is None:
                P.add(eng, lambda e: e.tensor_scalar(out, in0, s1, None, op0), reads=reads, writes=[out])
            else:
                P.add(eng, lambda e: e.tensor_scalar(out, in0, s1, s2, op0, op1), reads=reads, writes=[out])

        def recip(out, in_):
            P.add('dve', lambda e: e.reciprocal(out, in_), reads=[in_], writes=[out])

        def memset(ap, val, eng='dve'):
            P.add(eng, lambda e: e.memset(ap, val), writes=[ap])

        def bcast(ap, pattern):
            a = ap.ap
            return bass.AP(ap.tensor, ap.offset, [list(a[0])] + [list(x) for x in pattern])

        def r13_load(src):
            i = r13_i[0] % R13_N
            r13_i[0] += 1
            P.dma('pool', r13[i], src.rearrange("p (a b) -> p a b", b=128), r13_slots[i])
            P.seal(r13_slots[i])
            return r13[i]

        def r2_load(src, shape):
            i = r2_i[0] % R2_N
            r2_i[0] += 1
            v = view(R2_OFF + i * R2_SLOT, shape, BF16)
            if len(src.shape) == 2:
                src = src.rearrange("p (a b) -> p a b", b=shape[1])
            P.dma('pool', v, src, r2_slots[i])
            P.seal(r2_slots[i])
            return v

        s_x = P.slot()
        for c in range(KC):
            P.dma('sp', xT[:, c, :], xT_in[c * 128:(c + 1) * 128, :], s_x)
        P.seal(s_x)
        s_v = P.slot()
        P.dma('sp', vecs, vecs_in, s_v)
        P.seal(s_v)
        s_m = P.slot()
        P.dma('pool', mats, mats_in.rearrange("p (a b) -> p a b", b=128), s_m)
        P.seal(s_m)
        memset(epsT, EPS)
        cT = vecs[:, V_CT:V_CT + 16].rearrange("p (a b) -> p a b", b=2)
        act(scT, cT, AF.Silu)

        def normg(L, i):
            o = V_NORMG + (L * 4 + i) * 8
            return vecs[:, o:o + 8]

        def rstd_from(ss, n, scale):
            rs = rstd_r.next()[:, 0:n]
            act(rs, ss, AF.Sqrt, bias=epsT[:, 0:1], scale=scale)
            recip(rs, rs)
            return rs

        def prenorm(cols, n, A, B, mi, out):
            act(xsq[:, :, 0:n], xT[:, :, cols:cols + n], AF.Square)
            ss = psB.next()[:, 0:n]
            for c in range(KC):
                mm(ss, ones_m, xsq[:, c, 0:n], c == 0, c == KC - 1)
            rs = rstd_from(ss, n, 1.0 / D)
            for c in range(KC):
                t = nt_r.next()[:, 0:n]
                stt(t, xT[:, c, cols:cols + n], A[:, c, mi:mi + 1], rs, ALU.mult, ALU.mult)
                act(out[:, c, 0:n], t, AF.Identity, bias=B[:, c, mi:mi + 1])

        def evac_y(y_ps, ybuf_c, ss, n, first, last, bias=None):
            if bias is None:
                act(ybuf_c, y_ps, AF.Copy)
                q = ysq_r.next()[:, 0:n]
                act(q, y_ps, AF.Square)
            else:
                act(ybuf_c, y_ps, AF.Identity, bias=bias)
                q = ysq_r.next()[:, 0:n]
                act(q, y_ps, AF.Square, bias=bias)
            mm(ss, ones_m, q, first, last)

        def postnorm_residual(cols, n, ybuf, G, mi, ss):
            rs = rstd_from(ss, n, 1.0 / D)
            for c in range(KC):
                t = nt_r.next()[:, 0:n]
                stt(t, ybuf[:, c, 0:n], G[:, c, mi:mi + 1], rs, ALU.mult, ALU.mult)
                tt(xT[:, c, cols:cols + n], xT[:, c, cols:cols + n], t, ALU.add)

        def modulation(L):
            mps = psB.next()[:, 0:96]
            for j in range(48):
                w = r13_load(wmod_in[L * 48 + j])
                for kc in range(KC):
                    mm(mps[:, 2 * j:2 * j + 2], w[:, kc, :], scT[:, kc, :], kc == 0, kc == KC - 1)
            bm = vecs[:, V_BMOD + L * 48:V_BMOD + (L + 1) * 48]
            tt(mod_sb, mps.rearrange("p (a b) -> p a b", b=2), bcast(bm, [[1, 48], [0, 2]]), ALU.add)
            g0 = bcast(normg(L, 0), [[1, 8], [0, 2]])
            g1 = bcast(normg(L, 1), [[1, 8], [0, 2]])
            g2 = bcast(normg(L, 2), [[1, 8], [0, 2]])
            g3 = bcast(normg(L, 3), [[1, 8], [0, 2]])
            stt(A_mix, mod_sb[:, 8:16, :], 1.0, g0, ALU.add, ALU.mult)
            tt(G_mix, mod_sb[:, 16:24, :], g1, ALU.mult)
            stt(A_ffn, mod_sb[:, 32:40, :], 1.0, g2, ALU.add, ALU.mult)
            tt(G_ffn, mod_sb[:, 40:48, :], g3, ALU.mult)

        B_mix = mod_sb[:, 0:8, :]
        B_ffn = mod_sb[:, 24:32, :]

        def ffn(L, sbs):
            g = view(AR_OFF, [MFF, 768], BF16)
            hT = view(AR_OFF + 33792, [KC, 768], BF16)
            ybuf = view(AR_OFF + 46080, [KC, 768], F32)
            for sb in sbs:
                sb0 = sb[0][0]
                for (cols, n) in sb:
                    mi = 0 if cols < SEQ else 1
                    prenorm(cols, n, A_ffn, B_ffn, mi, hT[:, :, cols - sb0:cols - sb0 + n])
                AH = 3
                pend = []
                for m in range(min(AH, MFF)):
                    pend.append((r13_load(w1_in[L * MFF + m]), r13_load(w3_in[L * MFF + m])))
                w2v = [None] * 8
                w2v[0] = r2_load(w2_in[L * 8 + 0], [MFF, 128])
                for m in range(MFF):
                    w1v, w3v = pend.pop(0)
                    for (cols, n) in sb:
                        lc = cols - sb0
                        h1 = psA.next()[:, 0:n]
                        h3 = psA.next()[:, 0:n]
                        for kc in range(KC):
                            mm(h1, w1v[:, kc, :], hT[:, kc, lc:lc + n], kc == 0, kc == KC - 1)
                        for kc in range(KC):
                            mm(h3, w3v[:, kc, :], hT[:, kc, lc:lc + n], kc == 0, kc == KC - 1)
                        s = stmp_r.next()[:, 0:n]
                        act(s, h1, AF.Silu)
                        tt(g[:, m, lc:lc + n], h3, s, ALU.mult)
                    if m + AH < MFF:
                        pend.append((r13_load(w1_in[L * MFF + m + AH]), r13_load(w3_in[L * MFF + m + AH])))
                sss = [psB.next() for _ in sb]
                w2v[1] = r2_load(w2_in[L * 8 + 1], [MFF, 128])
                for mo in range(8):
                    for bi, (cols, n) in enumerate(sb):
                        lc = cols - sb0
                        y = psA.next()[:, 0:n]
                        for kc in range(MFF):
                            mm(y, w2v[mo][:, kc, :], g[:, kc, lc:lc + n], kc == 0, kc == MFF - 1)
                        evac_y(y, ybuf[:, mo, lc:lc + n], sss[bi][:, 0:n], n, mo == 0, mo == 7)
                    if mo + 2 < 8:
                        w2v[mo + 2] = r2_load(w2_in[L * 8 + mo + 2], [MFF, 128])
                for bi, (cols, n) in enumerate(sb):
                    lc = cols - sb0
                    mi = 0 if cols < SEQ else 1
                    postnorm_residual(cols, n, ybuf[:, :, lc:lc + n], G_ffn, mi, sss[bi][:, 0:n])

        def conv_layer(L, j):
            vb = V_CONV + j * CONV_SZ
            b_pw1 = vecs[:, vb:vb + 16]
            wdw = vecs[:, vb + 16:vb + 16 + 248].rearrange("p (a b) -> p a b", b=31)
            b_dw = vecs[:, vb + 264:vb + 272]
            ln_g = vecs[:, vb + 272:vb + 280]
            ln_b = vecs[:, vb + 280:vb + 288]
            b_pw2 = vecs[:, vb + 288:vb + 296]
            BW = 813
            hTp = view(AR_OFF, [KC, 816], BF16)
            glu1 = [view(AR_OFF + 13056 + i * 1632, [816], BF16) for i in range(2)]
            diag = [view(AR_OFF + 16320 + i * 7936, [31, 128], BF16) for i in range(2)]
            vbuf = view(AR_OFF + 32192, [KC, 768], F32)
            zb = view(AR_OFF + 56768, [KC, 256], BF16)
            ybuf = view(AR_OFF + 60864, [KC, 256], F32)
            stat = [view(AR_OFF + 69056 + i * 1024, [256], F32) for i in range(4)]
            parts = [
                (None, [(0, 392, 15), (392, 391, 407)], [(15, 392), (407, 391)], [(0, 15)],
                 [(0, 384, 0, 0), (384, 384, 384, 384)]),
                ((753, 15, 0), [(768, 384, 15), (1152, 399, 399)], [(0, 399), (399, 399)], [],
                 [(768, 384, 0, 0), (1152, 384, 384, 384)]),
                ((1521, 15, 0), [(1536, 249, 15), (1785, 263, 264), (2048, 256, 542)], [(0, 264), (264, 263), (542, 256)],
                 [(527, 15), (798, 15)], [(1536, 512, 0, 0), (2048, 256, 527, 512)]),
            ]
            for pi, (halo, nblocks, pblocks, zeros, oblocks) in enumerate(parts):
                for (t0, n, bc0) in nblocks:
                    mi = 0 if t0 < SEQ else 1
                    prenorm(t0, n, A_mix, B_mix, mi, hTp[:, :, bc0:bc0 + n])
                for m in range(KC):
                    wa = r13_load(pw1_in[j * 16 + m])
                    wg = r13_load(pw1_in[j * 16 + 8 + m])
                    gl = glu1[m % 2]
                    for (z0, zn) in zeros:
                        memset(gl[:, z0:z0 + zn], 0.0)
                    for (bc0, n) in pblocks:
                        a_ps = psA.next()[:, 0:n]
                        g_ps = psA.next()[:, 0:n]
                        for kc in range(KC):
                            mm(a_ps, wa[:, kc, :], hTp[:, kc, bc0:bc0 + n], kc == 0, kc == KC - 1)
                        for kc in range(KC):
                            mm(g_ps, wg[:, kc, :], hTp[:, kc, bc0:bc0 + n], kc == 0, kc == KC - 1)
                        sg = stmp_r.next()[:, 0:n]
                        act(sg, g_ps, AF.Sigmoid, bias=b_pw1[:, 8 + m:9 + m])
                        stt(gl[:, bc0:bc0 + n], a_ps, b_pw1[:, m:m + 1], sg, ALU.add, ALU.mult)
                    dg = diag[m % 2]
                    tt(dg, bcast(ident, [[0, 31], [1, 128]]), bcast(wdw[:, m, :], [[1, 31], [0, 128]]), ALU.mult)
                    for (t0, n, bt0, lc) in oblocks:
                        v_ps = psA.next()[:, 0:n]
                        for tap in range(31):
                            mm(v_ps, dg[:, tap, :], gl[:, bt0 + tap:bt0 + tap + n], tap == 0, tap == 30)
                        act(vbuf[:, m, lc:lc + n], v_ps, AF.Identity, bias=b_dw[:, m:m + 1])
                if pi + 1 < len(parts):
                    (t0, n, bc0) = parts[pi + 1][0]
                    prenorm(t0, n, A_mix, B_mix, 0, hTp[:, :, bc0:bc0 + n])
                w2p = [r13_load(pw2_in[j * 8 + mo]) for mo in range(8)]
                for (t0, n, bt0, lc0) in oblocks:
                    for sub in range(0, n, 256):
                        tk = t0 + sub
                        lc = lc0 + sub
                        nn = min(256, n - sub)
                        mi = 0 if tk < SEQ else 1
                        vs = xsq[:, :, 0:nn]
                        vq = xsq[:, :, 256:256 + nn]
                        act(vs, vbuf[:, :, lc:lc + nn], AF.Copy)
                        act(vq, vbuf[:, :, lc:lc + nn], AF.Square)
                        s1 = psB.next()[:, 0:nn]
                        s2 = psB.next()[:, 0:nn]
                        for c in range(KC):
                            mm(s1, ones_m, vs[:, c, :], c == 0, c == KC - 1)
                        for c in range(KC):
                            mm(s2, ones_m, vq[:, c, :], c == 0, c == KC - 1)
                        mean, msq, rs, nmr = stat[0][:, 0:nn], stat[1][:, 0:nn], stat[2][:, 0:nn], stat[3][:, 0:nn]
                        act(mean, s1, AF.Copy, scale=1.0 / D)
                        tt(msq, mean, mean, ALU.mult)
                        stt(msq, s2, 1.0 / D, msq, ALU.mult, ALU.subtract)
                        act(rs, msq, AF.Sqrt, bias=epsT[:, 0:1], scale=1.0)
                        recip(rs, rs)
                        stt(nmr, mean, -1.0, rs, ALU.mult, ALU.mult)
                        for c in range(KC):
                            t = nt_r.next()[:, 0:nn]
                            tt(t, vbuf[:, c, lc:lc + nn], rs, ALU.mult)
                            tt(t, t, nmr, ALU.add)
                            act(zb[:, c, 0:nn], t, AF.Silu, bias=ln_b[:, c:c + 1], scale=ln_g[:, c:c + 1])
                        ss = psB.next()[:, 0:nn]
                        for mo in range(8):
                            y = psA.next()[:, 0:nn]
                            for kc in range(KC):
                                mm(y, w2p[mo][:, kc, :], zb[:, kc, 0:nn], kc == 0, kc == KC - 1)
                            evac_y(y, ybuf[:, mo, 0:nn], ss, nn, mo == 0, mo == 7, bias=b_pw2[:, mo:mo + 1])
                        postnorm_residual(tk, nn, ybuf, G_mix, mi, ss)

        def attn_layer(L, j, need_ctx):
            gq = vecs[:, V_ATTN + 2 * j:V_ATTN + 2 * j + 1]
            gk = vecs[:, V_ATTN + 2 * j + 1:V_ATTN + 2 * j + 2]
            qT = view(AR_OFF, [KC, T], BF16)
            kT = view(AR_OFF + 36864, [2, T], BF16)
            Vt = view(AR_OFF + 46080, [18, 4, 128], BF16)
            hTb = view(AR_OFF + 64512, [KC, 512], BF16)
            ybuf = view(AR_OFF + 64512, [KC, 256], F32)
            ropeb = view(AR_OFF + 72704, [2, 512], F32)
            assert 76800 <= PH_SZ
            PT_r = _Rot([view(SCR_OFF + i * 1024, [512], BF16) for i in range(4)])
            ssum_r = _Rot([view(SCR_OFF + 8192 + i * 2048, [512], F32) for i in range(2)])
            qg_r = _Rot([view(SCR_OFF + 4096 + i * 1024, [512], BF16) for i in range(2)])
            s_rope = P.slot()
            wqp = [r13_load(wq_in[j * 8 + c]) for c in range(8)]
            wkp = r2_load(wk_in[j * 2:(j + 1) * 2].rearrange("a p (k f) -> p a k f", f=128), [2, KC, 128])
            wvp = r2_load(wv_in[j], [KC, 256])
            import os
            STOP = os.environ.get("ATTN_STOP", "")
            memset(Vt[:, :, :, 64:128], 1.0)
            if STOP == "memset":
                return
            blocks = [(0, 512), (512, 512), (1024, 512), (1536, 512), (2048, 256)]

            def qk_norm_rope(raw, n, gain, rope, out):
                sq = ysq_r.next()[:, 0:n]
                act(sq, raw, AF.Square)
                ss = psB.next()[:, 0:n]
                mm(ss, blk64, sq, True, True)
                if not rope:
                    rs = rstd_from(ss, n, 1.0 / 64)
                    stt(out, raw, gain, rs, ALU.mult, ALU.mult)
                    return
                t1 = nt_r.next()[:, 0:n]
                act(t1, raw, AF.Copy, scale=gain)
                qg = qg_r.next()[:, 0:n]
                P.add('dve', lambda e: e.tensor_copy(qg, t1), reads=[t1], writes=[qg])
                sw = psB.next()[:, 0:n]
                mm(sw, swapm, qg, True, True)
                rs = rstd_from(ss, n, 1.0 / 64)
                t2 = nt_r.next()[:, 0:n]
                tt(t1, t1, ropeb[:, 0, 0:n], ALU.mult)
                tt(t2, sw, ropeb[:, 1, 0:n], ALU.mult)
                tt(t1, t1, t2, ALU.add)
                tt(out, t1, rs, ALU.mult)

            for (cols, n) in blocks:
                latent = cols < SEQ
                mi = 0 if latent else 1
                prenorm(cols, n, A_mix, B_mix, mi, hTb)
                if latent:
                    P.dma('sp', ropeb[:, :, 0:n], rope_in[:, :, cols:cols + n], s_rope)
                    P.seal(s_rope)
                if STOP == "pre":
                    return
                if latent or need_ctx:
                    for c in range(KC):
                        if STOP == "q1" and c == 1:
                            return
                        raw = psA.next()[:, 0:n]
                        for kc in range(KC):
                            mm(raw, wqp[c][:, kc, :], hTb[:, kc, 0:n], kc == 0, kc == KC - 1)
                        qk_norm_rope(raw, n, gq, latent, qT[:, c, cols:cols + n])
                for c2 in range(2):
                    raw = psA.next()[:, 0:n]
                    for kc in range(KC):
                        mm(raw, wkp[:, c2, kc, :], hTb[:, kc, 0:n], kc == 0, kc == KC - 1)
                    qk_norm_rope(raw, n, gk, latent, kT[:, c2, cols:cols + n])
                if STOP == "k":
                    return
                for ti in range(n // 128):
                    tile_i = cols // 128 + ti
                    vps = psA.next()[:, 0:256]
                    for kc in range(KC):
                        mm(vps, hTb[:, kc, ti * 128:(ti + 1) * 128], wvp[:, kc, :], kc == 0, kc == KC - 1)
                    act(Vt[:, tile_i, :, 0:64], vps.rearrange("p (a b) -> p a b", b=64), AF.Copy)
                if STOP == "proj1":
                    return
            if STOP == "proj":
                return
            wop = [r13_load(wo_in[j * 8 + mo]) for mo in range(8)]
            qblocks = [(0, 512, list(range(18))), (512, 512, list(range(18))), (1024, 512, list(range(18))),
                       (1536, 512, list(range(18)))]
            if need_ctx:
                qblocks.append((2048, 256, [16, 17]))
            for c in range(KC):
                for hh in range(2):
                    kc2 = c // 4
                    jj = 2 * kc2 + hh
                    pr = slice(64 * hh, 64 * hh + 64)
                    for (qc, n, tiles) in qblocks:
                        O = psB.next()[:, 0:n]
                        S_list = {}

                        def issue_S(idx):
                            kt = tiles[idx]
                            S = psA.next()[:, 0:n]
                            mm(S, kT[pr, kc2, kt * 128:(kt + 1) * 128], qT[pr, c, qc:qc + n], True, True)
                            S_list[idx] = S
                        LA = 3
                        for idx in range(min(LA, len(tiles))):
                            issue_S(idx)
                        for idx, kt in enumerate(tiles):
                            PT = PT_r.next()[:, 0:n]
                            act(PT, S_list.pop(idx), AF.Exp, scale=0.125)
                            if idx + LA < len(tiles):
                                issue_S(idx + LA)
                            mm(O, Vt[:, kt, jj, :], PT, idx == 0, idx == len(tiles) - 1)
                        ssum = ssum_r.next()
                        act(ssum[0:64, 0:n], O[64:128, 0:n], AF.Copy)
                        recip(ssum[0:64, 0:n], ssum[0:64, 0:n])
                        tt(qT[pr, c, qc:qc + n], O[0:64, 0:n], ssum[0:64, 0:n], ALU.mult)
                        if STOP == "score1":
                            return
            if STOP == "score":
                return
            oblocks = [(i * 256, 256) for i in range(8)] + ([(2048, 256)] if need_ctx else [])
            for (cols, n) in oblocks:
                mi = 0 if cols < SEQ else 1
                ss = psB.next()[:, 0:n]
                for mo in range(8):
                    y = psA.next()[:, 0:n]
                    for kc in range(KC):
                        mm(y, wop[mo][:, kc, :], qT[:, kc, cols:cols + n], kc == 0, kc == KC - 1)
                    evac_y(y, ybuf[:, mo, :], ss, n, mo == 0, mo == 7)
                postnorm_residual(cols, n, ybuf, G_mix, mi, ss)

        for L in range(n_layers):
            need_ctx = L < DEPTH - 1
            modulation(L)
            if dbg_stage == ("mod", L):
                break
            if L % 2 == 0:
                conv_layer(L, L // 2)
            else:
                attn_layer(L, L // 2, need_ctx)
            if dbg_stage == ("mix", L):
                break
            sbs = [[(0, 384), (384, 384)], [(768, 384), (1152, 384)]]
            sbs.append([(1536, 512), (2048, 256)] if need_ctx else [(1536, 512)])
            ffn(L, sbs)

        s_o = P.slot()
        for c in range(KC):
            P.dma('sp', yT_out[c * 128:(c + 1) * 128, :], xT[:, c, 0:OUTC], s_o)
        P.seal(s_o)
        P.emit(final_slots=[s_o])
    return nc, P


def _vecT(v):
    v = np.asarray(v, np.float32)
    lead = v.shape[:-1]
    r = v.reshape(lead + (v.shape[-1] // 128, 128))
    r = np.moveaxis(r, -1, 0)
    return np.ascontiguousarray(r)


def _pieces(W):
    K, N = W.shape
    r = W.reshape(K // 128, 128, N // 128, 128).transpose(2, 1, 0, 3)
    return np.ascontiguousarray(r.reshape(N // 128, 128, (K // 128) * 128))


def _head_perm():
    perm = np.zeros(1024, np.int64)
    for c in range(8):
        for hh in range(2):
            head = (c if c < 4 else c + 4) + 4 * hh
            perm[c * 128 + hh * 64:c * 128 + hh * 64 + 64] = head * 64 + np.arange(64)
    return perm


def _rope_tables():
    pairs = 16
    freqs = (np.float32(10000.0) ** (-np.arange(pairs, dtype=np.float32) / np.float32(pairs))).astype(np.float32)
    t = np.arange(SEQ)
    row = (t // 64).astype(np.float32)
    col = (t % 64).astype(np.float32)
    ang = np.concatenate([row[:, None] * freqs[None, :], col[:, None] * freqs[None, :]], axis=-1).astype(np.float32)
    cos = np.cos(ang).astype(np.float32)
    sin = np.sin(ang).astype(np.float32)
    out = np.zeros((128, 2, SEQ), np.float32)
    for p in range(128):
        d = p % 64
        i = d // 2
        out[p, 0, :] = cos[:, i]
        out[p, 1, :] = -sin[:, i] if d % 2 == 0 else sin[:, i]
    return out


def _const_mats():
    m = np.zeros((128, 4, 128), np.float32)
    m[:, 0, :] = np.eye(128, dtype=np.float32)
    m[:, 1, :] = 1.0
    m[0:64, 2, 0:64] = 1.0
    m[64:128, 2, 64:128] = 1.0
    for k in range(128):
        m[k, 3, k ^ 1] = 1.0
    return np.ascontiguousarray(m.reshape(128, 512))


def prepare_inputs(x, c, ctx, c_ctx, w_mod, b_mod, norm_g,
                   conv_w_pw1, conv_b_pw1, conv_w_dw, conv_b_dw, conv_ln_g, conv_ln_b, conv_w_pw2, conv_b_pw2,
                   attn_wq, attn_wk, attn_wv, attn_wo, attn_q_g, attn_k_g, ffn_w1, ffn_w3, ffn_w2):
    f = lambda a: np.asarray(a, np.float32)
    x, c, ctx, c_ctx = f(x), f(c), f(ctx), f(c_ctx)
    shared = {}
    shared["mats"] = _const_mats()
    shared["rope"] = _rope_tables()
    shared["wmod"] = np.concatenate([_pieces(f(w_mod[l])) for l in range(DEPTH)], axis=0)
    shared["pw1"] = np.concatenate([_pieces(f(conv_w_pw1[j])) for j in range(2)], axis=0)
    shared["pw2"] = np.concatenate([_pieces(f(conv_w_pw2[j])) for j in range(2)], axis=0)
    perm = _head_perm()
    shared["wq"] = np.concatenate([_pieces(f(attn_wq[j])[:, perm]) for j in range(2)], axis=0)
    shared["wk"] = np.concatenate([_pieces(f(attn_wk[j])) for j in range(2)], axis=0)
    shared["wv"] = np.stack([np.ascontiguousarray(f(attn_wv[j]).reshape(8, 128, 256).transpose(1, 0, 2).reshape(128, 2048))
                             for j in range(2)], axis=0)
    shared["wo"] = np.concatenate([_pieces(f(attn_wo[j])[perm, :]) for j in range(2)], axis=0)
    shared["w1"] = np.concatenate([_pieces(f(ffn_w1[l])) for l in range(DEPTH)], axis=0)
    shared["w3"] = np.concatenate([_pieces(f(ffn_w3[l])) for l in range(DEPTH)], axis=0)
    shared["w2"] = np.concatenate([_pieces(f(ffn_w2[l])) for l in range(DEPTH)], axis=0)
    vec_common = np.zeros((128, NV), np.float32)
    vec_common[:, V_BMOD:V_BMOD + 192] = _vecT(f(b_mod).reshape(DEPTH, 6144)).reshape(128, 192)
    vec_common[:, V_NORMG:V_NORMG + 128] = _vecT(f(norm_g)).reshape(128, 128)
    for j in range(2):
        o = V_CONV + j * CONV_SZ
        vec_common[:, o:o + 16] = _vecT(f(conv_b_pw1[j]))
        vec_common[:, o + 16:o + 264] = _vecT(f(conv_w_dw[j])).transpose(0, 2, 1).reshape(128, 248)
        vec_common[:, o + 264:o + 272] = _vecT(f(conv_b_dw[j]))
        vec_common[:, o + 272:o + 280] = _vecT(f(conv_ln_g[j]))
        vec_common[:, o + 280:o + 288] = _vecT(f(conv_ln_b[j]))
        vec_common[:, o + 288:o + 296] = _vecT(f(conv_b_pw2[j]))
        vec_common[:, V_ATTN + 2 * j] = np.tile(f(attn_q_g[j]), 2)
        vec_common[:, V_ATTN + 2 * j + 1] = np.tile(f(attn_k_g[j]), 2)
    in_maps = []
    ccT = _vecT(c_ctx)
    for b in range(NCORES):
        m = dict(shared)
        m["xT"] = np.ascontiguousarray(np.concatenate([x[b].T, ctx[b].T], axis=1))
        v = vec_common.copy()
        cb = _vecT(c[b])
        v[:, V_CT:V_CT + 16] = np.stack([cb, ccT], axis=-1).reshape(128, 16)
        m["vecs"] = v
        in_maps.append(m)
    return in_maps


_CACHE = {}


def kernel(**inputs):
    in_maps = prepare_inputs(**inputs)
    if "nc" not in _CACHE:
        _CACHE["nc"] = build_program()[0]
    nc = _CACHE["nc"]
    res = run_bass_kernel_spmd(nc, in_maps, core_ids=list(range(NCORES)))
    out = np.stack([np.ascontiguousarray(r["yT"].T) for r in res.results], axis=0)
    return out.astype(np.float32)
```

```python
import numpy as np
import concourse.bass as bass
import concourse.mybir as mybir
from concourse.bass_utils import run_bass_kernel_spmd

F32 = mybir.dt.float32
BF16 = mybir.dt.bfloat16
U8 = mybir.dt.uint8
AF = mybir.ActivationFunctionType
ALU = mybir.AluOpType

_DTS = {F32: 4, BF16: 2, U8: 1}


def dtsize(dt):
    return _DTS[dt]


class _Op:
    __slots__ = ("eng", "fn", "slot", "deps", "signal", "tick", "idx")


class _Slot:
    __slots__ = ("sem", "count", "pending")


class Prog:
    ENGS = ("pe", "act", "dve", "pool", "sp")

    def __init__(self, nc):
        self.nc = nc
        self.ops = []
        self.recs = {}
        self.slots = []

    def slot(self):
        s = _Slot()
        s.sem = None
        s.count = 0
        s.pending = []
        self.slots.append(s)
        return s

    @staticmethod
    def _region(ap):
        t = ap.tensor
        tn = type(t).__name__
        if tn.startswith("DRam"):
            return None
        a = ap.ap
        es = _DTS[ap.dtype]
        pstride, pn = a[0]
        off = ap.offset
        p0 = off // pstride
        f0 = off - p0 * pstride
        ext = 1
        for s, c in a[1:]:
            ext += (c - 1) * abs(s)
        return (t.name, f0 * es, (f0 + ext) * es, p0, p0 + pn)

    def add(self, eng, fn, reads=(), writes=(), slot=None):
        op = _Op()
        op.eng = eng
        op.fn = fn
        op.slot = slot
        op.signal = False
        op.tick = 0
        op.idx = len(self.ops)
        deps = {}
        rregs = [r for r in (self._region(a) for a in reads) if r is not None]
        wregs = [r for r in (self._region(a) for a in writes) if r is not None]
        for (name, lo, hi, plo, phi) in rregs:
            for r in self.recs.get(name, ()):
                if r[5] and r[0] < hi and lo < r[1] and r[2] < phi and plo < r[3]:
                    deps[r[4]] = True
        for (name, lo, hi, plo, phi) in wregs:
            for r in self.recs.get(name, ()):
                if r[0] < hi and lo < r[1] and r[2] < phi and plo < r[3]:
                    if r[4] not in deps:
                        deps[r[4]] = False
        deps.pop(op.idx, None)
        op.deps = deps
        self.ops.append(op)
        for (name, lo, hi, plo, phi) in wregs:
            lst = self.recs.setdefault(name, [])
            lst[:] = [r for r in lst if not (lo <= r[0] and r[1] <= hi and plo <= r[2] and r[3] <= phi)]
            lst.append([lo, hi, plo, phi, op.idx, True])
        for (name, lo, hi, plo, phi) in rregs:
            lst = self.recs.setdefault(name, [])
            if slot is None:
                lst[:] = [r for r in lst if not ((not r[5]) and self.ops[r[4]].eng == eng and self.ops[r[4]].slot is None
                                                  and lo <= r[0] and r[1] <= hi and plo <= r[2] and r[3] <= phi)]
            lst.append([lo, hi, plo, phi, op.idx, False])
        return op

    def dma(self, eng, out, in_, slot):
        op = self.add(eng, lambda e: e.dma_start(out=out, in_=in_), reads=[in_], writes=[out], slot=slot)
        slot.count += 16
        slot.pending.append(op)
        return op

    def seal(self, slot):
        for op in slot.pending:
            op.tick = slot.count
        slot.pending = []

    def emit(self, final_slots=()):
        nc = self.nc
        ops = self.ops
        for s in self.slots:
            assert not s.pending, "unsealed DMA slot"
        for op in ops:
            best = {}
            for d, raw in op.deps.items():
                p = ops[d]
                if p.slot is not None:
                    key = ("d", id(p.slot))
                else:
                    if p.eng == op.eng and op.slot is None:
                        if p.eng == "pe" or not raw:
                            continue
                    key = ("e", p.eng)
                if key not in best or best[key].idx < p.idx:
                    best[key] = p
            op.deps = best
            for key, p in best.items():
                if p.slot is None:
                    p.signal = True
        cnt = {e: 0 for e in self.ENGS}
        for op in ops:
            if op.slot is None and op.signal:
                cnt[op.eng] += 1
                op.tick = cnt[op.eng]
        import contextlib
        with contextlib.ExitStack() as st:
            esem = {e: st.enter_context(nc.semaphore("S_" + e)) for e in self.ENGS}
            for i, s in enumerate(self.slots):
                s.sem = st.enter_context(nc.semaphore("D%d" % i))
            block = st.enter_context(nc.Block())
            self.n_waits = 0

            def stream(engname, e):
                waited = {}
                for op in ops:
                    if op.eng != engname:
                        continue
                    need = {}
                    for key, p in op.deps.items():
                        sem = p.slot.sem if p.slot is not None else esem[p.eng]
                        need[key] = (sem, p.tick)
                    todo = []
                    for key, (sem, val) in need.items():
                        if val > waited.get(key, 0):
                            todo.append((sem, val))
                            waited[key] = val
                    for (sem, val) in todo[1:]:
                        e.wait_ge(sem, val)
                        self.n_waits += 1
                    ins = op.fn(e)
                    if todo:
                        ins._wait_ge(todo[0][0], todo[0][1])
                    if op.slot is not None:
                        ins.then_inc(op.slot.sem, 16)
                    elif op.signal:
                        ins.then_inc(esem[op.eng], 1)
                if engname == "sp":
                    for s in final_slots:
                        e.wait_ge(s.sem, s.count)

            @block.tensor
            def _(e):
                stream("pe", e)

            @block.scalar
            def _(e):
                stream("act", e)

            @block.vector
            def _(e):
                stream("dve", e)

            @block.gpsimd
            def _(e):
                stream("pool", e)

            @block.sync
            def _(e):
                stream("sp", e)


D = 1024
KC = 8
SEQ = 2048
CTXL = 256
T = SEQ + CTXL
DFF = 2816
MFF = 22
DEPTH = 4
EPS = 1e-6
NCORES = 8

V_BMOD = 0
V_NORMG = V_BMOD + 192
V_CONV = V_NORMG + 128
CONV_SZ = 296
V_ATTN = V_CONV + 2 * CONV_SZ
V_CT = V_ATTN + 4
NV = V_CT + 16

XT_OFF = 0
XT_SZ = KC * T * 4
R13_OFF = XT_OFF + XT_SZ
R13_SLOT = 2048
R13_N = 8
R2_OFF = R13_OFF + R13_SLOT * R13_N
R2_SLOT = 5632
R2_N = 2
CONST_OFF = R2_OFF + R2_SLOT * R2_N
CONST_SZ = 7168
AR_OFF = CONST_OFF + CONST_SZ
TOTAL = 210944
SCR_SZ = 24576
SCR_OFF = TOTAL - SCR_SZ
PH_SZ = SCR_OFF - AR_OFF


class _Rot:
    def __init__(self, items):
        self.items = items
        self.i = 0

    def next(self):
        v = self.items[self.i % len(self.items)]
        self.i += 1
        return v


def build_program(n_layers=DEPTH, dbg_stage=None, full_out=False):
    nc = bass.Bass("TRN2", target_bir_lowering=False)

    def dram(name, shape, kind="ExternalInput"):
        return nc.dram_tensor(name, shape, F32, kind=kind).ap()

    xT_in = dram("xT", [D, T])
    vecs_in = dram("vecs", [128, NV])
    mats_in = dram("mats", [128, 4 * 128])
    rope_in = dram("rope", [128, 2, SEQ])
    wmod_in = dram("wmod", [DEPTH * 48, 128, 1024])
    pw1_in = dram("pw1", [2 * 16, 128, 1024])
    pw2_in = dram("pw2", [2 * 8, 128, 1024])
    wq_in = dram("wq", [2 * 8, 128, 1024])
    wk_in = dram("wk", [2 * 2, 128, 1024])
    wv_in = dram("wv", [2, 128, 2048])
    wo_in = dram("wo", [2 * 8, 128, 1024])
    w1_in = dram("w1", [DEPTH * MFF, 128, 1024])
    w3_in = dram("w3", [DEPTH * MFF, 128, 1024])
    w2_in = dram("w2", [DEPTH * 8, 128, DFF])
    OUTC = T if full_out else SEQ
    yT_out = dram("yT", [D, OUTC], kind="ExternalOutput")

    P = Prog(nc)
    import contextlib
    st = contextlib.ExitStack()
    with st:
        ar = st.enter_context(nc.sbuf_tensor("ar", [128, TOTAL], U8))
        banks = [st.enter_context(nc.psum_tensor("pb%d" % i, [128, 512], F32)) for i in range(8)]
        psA = _Rot(banks[0:6])
        psB = _Rot(banks[6:8])

        def view(off, shape, dt):
            n = int(np.prod(shape))
            assert off % 4 == 0 and off + n * dtsize(dt) <= TOTAL, (off, shape)
            v = ar[:, off:off + n * dtsize(dt)].bitcast(dt)
            if len(shape) == 2:
                v = v.rearrange("p (a b) -> p a b", b=shape[1])
            elif len(shape) == 3:
                v = v.rearrange("p (a b c) -> p a b c", b=shape[1], c=shape[2])
            return v

        xT = view(XT_OFF, [KC, T], F32)
        r13 = [view(R13_OFF + i * R13_SLOT, [KC, 128], BF16) for i in range(R13_N)]
        r13_slots = [P.slot() for _ in range(R13_N)]
        r13_i = [0]
        r2_slots = [P.slot() for _ in range(R2_N)]
        r2_i = [0]
        co = [CONST_OFF]

        def calloc(shape, dt):
            n = int(np.prod(shape)) * dtsize(dt)
            n = (n + 3) // 4 * 4
            v = view(co[0], shape, dt)
            co[0] += n
            assert co[0] <= CONST_OFF + CONST_SZ
            return v

        vecs = calloc([NV], F32)
        mats = calloc([4, 128], BF16)
        ident, ones_m, blk64, swapm = mats[:, 0, :], mats[:, 1, :], mats[:, 2, :], mats[:, 3, :]
        scT = calloc([KC, 2], BF16)
        mod_sb = calloc([48, 2], F32)
        A_mix = calloc([KC, 2], F32)
        G_mix = calloc([KC, 2], F32)
        A_ffn = calloc([KC, 2], F32)
        G_ffn = calloc([KC, 2], F32)
        epsT = calloc([1], F32)

        xsq = view(SCR_OFF, [KC, 512], BF16)
        rstd_r = _Rot([view(SCR_OFF + 8192 + i * 2048, [512], F32) for i in range(2)])
        nt_r = _Rot([view(SCR_OFF + 12288 + i * 2048, [512], F32) for i in range(3)])
        ysq_r = _Rot([view(SCR_OFF + 18432 + i * 1024, [512], BF16) for i in range(2)])
        stmp_r = _Rot([view(SCR_OFF + 20480 + i * 2048, [512], F32) for i in range(2)])

        def mm(out, lhsT, rhs, start, stop):
            P.add('pe', lambda e: e.matmul(out, lhsT, rhs, start=start, stop=stop), reads=[lhsT, rhs], writes=[out])

        def act(out, in_, func, bias=None, scale=None):
            kw = {}
            reads = [in_]
            if bias is not None:
                kw['bias'] = bias
                if not isinstance(bias, float):
                    reads.append(bias)
            if scale is not None:
                kw['scale'] = scale
                if not isinstance(scale, float):
                    reads.append(scale)
            P.add('act', lambda e: e.activation(out, in_, func, **kw), reads=reads, writes=[out])

        def tt(out, in0, in1, op, eng='dve'):
            P.add(eng, lambda e: e.tensor_tensor(out, in0, in1, op), reads=[in0, in1], writes=[out])

        def stt(out, in0, scalar, in1, op0, op1, eng='dve'):
            reads = [in0, in1] + ([] if isinstance(scalar, float) else [scalar])
            P.add(eng, lambda e: e.scalar_tensor_tensor(out, in0, scalar, in1, op0, op1), reads=reads, writes=[out])

        def tsc(out, in0, s1, op0, s2=None, op1=None, eng='dve'):
            reads = [in0] + [s for s in (s1, s2) if s is not None and not isinstance(s, float)]
            if op1 is None:
                P.add(eng, lambda e: e.tensor_scalar(out, in0, s1, None, op0), reads=reads, writes=[out])
            else:
                P.add(eng, lambda e: e.tensor_scalar(out, in0, s1, s2, op0, op1), reads=reads, writes=[out])

        def recip(out, in_):
            P.add('dve', lambda e: e.reciprocal(out, in_), reads=[in_], writes=[out])

        def memset(ap, val, eng='dve'):
            P.add(eng, lambda e: e.memset(ap, val), writes=[ap])

        def bcast(ap, pattern):
            a = ap.ap
            return bass.AP(ap.tensor, ap.offset, [list(a[0])] + [list(x) for x in pattern])

        def r13_load(src):
            i = r13_i[0] % R13_N
            r13_i[0] += 1
            P.dma('pool', r13[i], src.rearrange("p (a b) -> p a b", b=128), r13_slots[i])
            P.seal(r13_slots[i])
            return r13[i]

        def r2_load(src, shape):
            i = r2_i[0] % R2_N
            r2_i[0] += 1
            v = view(R2_OFF + i * R2_SLOT, shape, BF16)
            if len(src.shape) == 2:
                src = src.rearrange("p (a b) -> p a b", b=shape[1])
            P.dma('pool', v, src, r2_slots[i])
            P.seal(r2_slots[i])
            return v

        s_x = P.slot()
        for c in range(KC):
            P.dma('sp', xT[:, c, :], xT_in[c * 128:(c + 1) * 128, :], s_x)
        P.seal(s_x)
        s_v = P.slot()
        P.dma('sp', vecs, vecs_in, s_v)
        P.seal(s_v)
        s_m = P.slot()
        P.dma('pool', mats, mats_in.rearrange("p (a b) -> p a b", b=128), s_m)
        P.seal(s_m)
        memset(epsT, EPS)
        cT = vecs[:, V_CT:V_CT + 16].rearrange("p (a b) -> p a b", b=2)
        act(scT, cT, AF.Silu)

        def normg(L, i):
            o = V_NORMG + (L * 4 + i) * 8
            return vecs[:, o:o + 8]

        def rstd_from(ss, n, scale):
            rs = rstd_r.next()[:, 0:n]
            act(rs, ss, AF.Sqrt, bias=epsT[:, 0:1], scale=scale)
            recip(rs, rs)
            return rs

        def prenorm(cols, n, A, B, mi, out):
            act(xsq[:, :, 0:n], xT[:, :, cols:cols + n], AF.Square)
            ss = psB.next()[:, 0:n]
            for c in range(KC):
                mm(ss, ones_m, xsq[:, c, 0:n], c == 0, c == KC - 1)
            rs = rstd_from(ss, n, 1.0 / D)
            for c in range(KC):
                t = nt_r.next()[:, 0:n]
                stt(t, xT[:, c, cols:cols + n], A[:, c, mi:mi + 1], rs, ALU.mult, ALU.mult)
                act(out[:, c, 0:n], t, AF.Identity, bias=B[:, c, mi:mi + 1])

        def evac_y(y_ps, ybuf_c, ss, n, first, last, bias=None):
            if bias is None:
                act(ybuf_c, y_ps, AF.Copy)
                q = ysq_r.next()[:, 0:n]
                act(q, y_ps, AF.Square)
            else:
                act(ybuf_c, y_ps, AF.Identity, bias=bias)
                q = ysq_r.next()[:, 0:n]
                act(q, y_ps, AF.Square, bias=bias)
            mm(ss, ones_m, q, first, last)

        def postnorm_residual(cols, n, ybuf, G, mi, ss):
            rs = rstd_from(ss, n, 1.0 / D)
            for c in range(KC):
                t = nt_r.next()[:, 0:n]
                stt(t, ybuf[:, c, 0:n], G[:, c, mi:mi + 1], rs, ALU.mult, ALU.mult)
                tt(xT[:, c, cols:cols + n], xT[:, c, cols:cols + n], t, ALU.add)

        def modulation(L):
            mps = psB.next()[:, 0:96]
            for j in range(48):
                w = r13_load(wmod_in[L * 48 + j])
                for kc in range(KC):
                    mm(mps[:, 2 * j:2 * j + 2], w[:, kc, :], scT[:, kc, :], kc == 0, kc == KC - 1)
            bm = vecs[:, V_BMOD + L * 48:V_BMOD + (L + 1) * 48]
            tt(mod_sb, mps.rearrange("p (a b) -> p a b", b=2), bcast(bm, [[1, 48], [0, 2]]), ALU.add)
            g0 = bcast(normg(L, 0), [[1, 8], [0, 2]])
            g1 = bcast(normg(L, 1), [[1, 8], [0, 2]])
            g2 = bcast(normg(L, 2), [[1, 8], [0, 2]])
            g3 = bcast(normg(L, 3), [[1, 8], [0, 2]])
            stt(A_mix, mod_sb[:, 8:16, :], 1.0, g0, ALU.add, ALU.mult)
            tt(G_mix, mod_sb[:, 16:24, :], g1, ALU.mult)
            stt(A_ffn, mod_sb[:, 32:40, :], 1.0, g2, ALU.add, ALU.mult)
            tt(G_ffn, mod_sb[:, 40:48, :], g3, ALU.mult)

        B_mix = mod_sb[:, 0:8, :]
        B_ffn = mod_sb[:, 24:32, :]

        def ffn(L, sbs):
            g = view(AR_OFF, [MFF, 768], BF16)
            hT = view(AR_OFF + 33792, [KC, 768], BF16)
            ybuf = view(AR_OFF + 46080, [KC, 768], F32)
            for sb in sbs:
                sb0 = sb[0][0]
                for (cols, n) in sb:
                    mi = 0 if cols < SEQ else 1
                    prenorm(cols, n, A_ffn, B_ffn, mi, hT[:, :, cols - sb0:cols - sb0 + n])
                AH = 3
                pend = []
                for m in range(min(AH, MFF)):
                    pend.append((r13_load(w1_in[L * MFF + m]), r13_load(w3_in[L * MFF + m])))
                w2v = [None] * 8
                w2v[0] = r2_load(w2_in[L * 8 + 0], [MFF, 128])
                for m in range(MFF):
                    w1v, w3v = pend.pop(0)
                    for (cols, n) in sb:
                        lc = cols - sb0
                        h1 = psA.next()[:, 0:n]
                        h3 = psA.next()[:, 0:n]
                        for kc in range(KC):
                            mm(h1, w1v[:, kc, :], hT[:, kc, lc:lc + n], kc == 0, kc == KC - 1)
                        for kc in range(KC):
                            mm(h3, w3v[:, kc, :], hT[:, kc, lc:lc + n], kc == 0, kc == KC - 1)
                        s = stmp_r.next()[:, 0:n]
                        act(s, h1, AF.Silu)
                        tt(g[:, m, lc:lc + n], h3, s, ALU.mult)
                    if m + AH < MFF:
                        pend.append((r13_load(w1_in[L * MFF + m + AH]), r13_load(w3_in[L * MFF + m + AH])))
                sss = [psB.next() for _ in sb]
                w2v[1] = r2_load(w2_in[L * 8 + 1], [MFF, 128])
                for mo in range(8):
                    for bi, (cols, n) in enumerate(sb):
                        lc = cols - sb0
                        y = psA.next()[:, 0:n]
                        for kc in range(MFF):
                            mm(y, w2v[mo][:, kc, :], g[:, kc, lc:lc + n], kc == 0, kc == MFF - 1)
                        evac_y(y, ybuf[:, mo, lc:lc + n], sss[bi][:, 0:n], n, mo == 0, mo == 7)
                    if mo + 2 < 8:
                        w2v[mo + 2] = r2_load(w2_in[L * 8 + mo + 2], [MFF, 128])
                for bi, (cols, n) in enumerate(sb):
                    lc = cols - sb0
                    mi = 0 if cols < SEQ else 1
                    postnorm_residual(cols, n, ybuf[:, :, lc:lc + n], G_ffn, mi, sss[bi][:, 0:n])

        def conv_layer(L, j):
            vb = V_CONV + j * CONV_SZ
            b_pw1 = vecs[:, vb:vb + 16]
            wdw = vecs[:, vb + 16:vb + 16 + 248].rearrange("p (a b) -> p a b", b=31)
            b_dw = vecs[:, vb + 264:vb + 272]
            ln_g = vecs[:, vb + 272:vb + 280]
            ln_b = vecs[:, vb + 280:vb + 288]
            b_pw2 = vecs[:, vb + 288:vb + 296]
            BW = 813
            hTp = view(AR_OFF, [KC, 816], BF16)
            glu1 = [view(AR_OFF + 13056 + i * 1632, [816], BF16) for i in range(2)]
            diag = [view(AR_OFF + 16320 + i * 7936, [31, 128], BF16) for i in range(2)]
            vbuf = view(AR_OFF + 32192, [KC, 768], F32)
            zb = view(AR_OFF + 56768, [KC, 256], BF16)
            ybuf = view(AR_OFF + 60864, [KC, 256], F32)
            stat = [view(AR_OFF + 69056 + i * 1024, [256], F32) for i in range(4)]
            parts = [
                (None, [(0, 392, 15), (392, 391, 407)], [(15, 392), (407, 391)], [(0, 15)],
                 [(0, 384, 0, 0), (384, 384, 384, 384)]),
                ((753, 15, 0), [(768, 384, 15), (1152, 399, 399)], [(0, 399), (399, 399)], [],
                 [(768, 384, 0, 0), (1152, 384, 384, 384)]),
                ((1521, 15, 0), [(1536, 249, 15), (1785, 263, 264), (2048, 256, 542)], [(0, 264), (264, 263), (542, 256)],
                 [(527, 15), (798, 15)], [(1536, 512, 0, 0), (2048, 256, 527, 512)]),
            ]
            for pi, (halo, nblocks, pblocks, zeros, oblocks) in enumerate(parts):
                for (t0, n, bc0) in nblocks:
                    mi = 0 if t0 < SEQ else 1
                    prenorm(t0, n, A_mix, B_mix, mi, hTp[:, :, bc0:bc0 + n])
                for m in range(KC):
                    wa = r13_load(pw1_in[j * 16 + m])
                    wg = r13_load(pw1_in[j * 16 + 8 + m])
                    gl = glu1[m % 2]
                    for (z0, zn) in zeros:
                        memset(gl[:, z0:z0 + zn], 0.0)
                    for (bc0, n) in pblocks:
                        a_ps = psA.next()[:, 0:n]
                        g_ps = psA.next()[:, 0:n]
                        for kc in range(KC):
                            mm(a_ps, wa[:, kc, :], hTp[:, kc, bc0:bc0 + n], kc == 0, kc == KC - 1)
                        for kc in range(KC):
                            mm(g_ps, wg[:, kc, :], hTp[:, kc, bc0:bc0 + n], kc == 0, kc == KC - 1)
                        sg = stmp_r.next()[:, 0:n]
                        act(sg, g_ps, AF.Sigmoid, bias=b_pw1[:, 8 + m:9 + m])
                        stt(gl[:, bc0:bc0 + n], a_ps, b_pw1[:, m:m + 1], sg, ALU.add, ALU.mult)
                    dg = diag[m % 2]
                    tt(dg, bcast(ident, [[0, 31], [1, 128]]), bcast(wdw[:, m, :], [[1, 31], [0, 128]]), ALU.mult)
                    for (t0, n, bt0, lc) in oblocks:
                        v_ps = psA.next()[:, 0:n]
                        for tap in range(31):
                            mm(v_ps, dg[:, tap, :], gl[:, bt0 + tap:bt0 + tap + n], tap == 0, tap == 30)
                        act(vbuf[:, m, lc:lc + n], v_ps, AF.Identity, bias=b_dw[:, m:m + 1])
                if pi + 1 < len(parts):
                    (t0, n, bc0) = parts[pi + 1][0]
                    prenorm(t0, n, A_mix, B_mix, 0, hTp[:, :, bc0:bc0 + n])
                w2p = [r13_load(pw2_in[j * 8 + mo]) for mo in range(8)]
                for (t0, n, bt0, lc0) in oblocks:
                    for sub in range(0, n, 256):
                        tk = t0 + sub
                        lc = lc0 + sub
                        nn = min(256, n - sub)
                        mi = 0 if tk < SEQ else 1
                        vs = xsq[:, :, 0:nn]
                        vq = xsq[:, :, 256:256 + nn]
                        act(vs, vbuf[:, :, lc:lc + nn], AF.Copy)
                        act(vq, vbuf[:, :, lc:lc + nn], AF.Square)
                        s1 = psB.next()[:, 0:nn]
                        s2 = psB.next()[:, 0:nn]
                        for c in range(KC):
                            mm(s1, ones_m, vs[:, c, :], c == 0, c == KC - 1)
                        for c in range(KC):
                            mm(s2, ones_m, vq[:, c, :], c == 0, c == KC - 1)
                        mean, msq, rs, nmr = stat[0][:, 0:nn], stat[1][:, 0:nn], stat[2][:, 0:nn], stat[3][:, 0:nn]
                        act(mean, s1, AF.Copy, scale=1.0 / D)
                        tt(msq, mean, mean, ALU.mult)
                        stt(msq, s2, 1.0 / D, msq, ALU.mult, ALU.subtract)
                        act(rs, msq, AF.Sqrt, bias=epsT[:, 0:1], scale=1.0)
                        recip(rs, rs)
                        stt(nmr, mean, -1.0, rs, ALU.mult, ALU.mult)
                        for c in range(KC):
                            t = nt_r.next()[:, 0:nn]
                            tt(t, vbuf[:, c, lc:lc + nn], rs, ALU.mult)
                            tt(t, t, nmr, ALU.add)
                            act(zb[:, c, 0:nn], t, AF.Silu, bias=ln_b[:, c:c + 1], scale=ln_g[:, c:c + 1])
                        ss = psB.next()[:, 0:nn]
                        for mo in range(8):
                            y = psA.next()[:, 0:nn]
                            for kc in range(KC):
                                mm(y, w2p[mo][:, kc, :], zb[:, kc, 0:nn], kc == 0, kc == KC - 1)
                            evac_y(y, ybuf[:, mo, 0:nn], ss, nn, mo == 0, mo == 7, bias=b_pw2[:, mo:mo + 1])
                        postnorm_residual(tk, nn, ybuf, G_mix, mi, ss)

        def attn_layer(L, j, need_ctx):
            gq = vecs[:, V_ATTN + 2 * j:V_ATTN + 2 * j + 1]
            gk = vecs[:, V_ATTN + 2 * j + 1:V_ATTN + 2 * j + 2]
            qT = view(AR_OFF, [KC, T], BF16)
            kT = view(AR_OFF + 36864, [2, T], BF16)
            Vt = view(AR_OFF + 46080, [18, 4, 128], BF16)
            hTb = view(AR_OFF + 64512, [KC, 512], BF16)
            ybuf = view(AR_OFF + 64512, [KC, 256], F32)
            ropeb = view(AR_OFF + 72704, [2, 512], F32)
            assert 76800 <= PH_SZ
            PT_r = _Rot([view(SCR_OFF + i * 1024, [512], BF16) for i in range(4)])
            ssum_r = _Rot([view(SCR_OFF + 8192 + i * 2048, [512], F32) for i in range(2)])
            qg_r = _Rot([view(SCR_OFF + 4096 + i * 1024, [512], BF16) for i in range(2)])
            s_rope = P.slot()
            wqp = [r13_load(wq_in[j * 8 + c]) for c in range(8)]
            wkp = r2_load(wk_in[j * 2:(j + 1) * 2].rearrange("a p (k f) -> p a k f", f=128), [2, KC, 128])
            wvp = r2_load(wv_in[j], [KC, 256])
            import os
            STOP = os.environ.get("ATTN_STOP", "")
            memset(Vt[:, :, :, 64:128], 1.0)
            if STOP == "memset":
                return
            blocks = [(0, 512), (512, 512), (1024, 512), (1536, 512), (2048, 256)]

            def qk_norm_rope(raw, n, gain, rope, out):
                sq = ysq_r.next()[:, 0:n]
                act(sq, raw, AF.Square)
                ss = psB.next()[:, 0:n]
                mm(ss, blk64, sq, True, True)
                if not rope:
                    rs = rstd_from(ss, n, 1.0 / 64)
                    stt(out, raw, gain, rs, ALU.mult, ALU.mult)
                    return
                t1 = nt_r.next()[:, 0:n]
                act(t1, raw, AF.Copy, scale=gain)
                qg = qg_r.next()[:, 0:n]
                P.add('dve', lambda e: e.tensor_copy(qg, t1), reads=[t1], writes=[qg])
                sw = psB.next()[:, 0:n]
                mm(sw, swapm, qg, True, True)
                rs = rstd_from(ss, n, 1.0 / 64)
                t2 = nt_r.next()[:, 0:n]
                tt(t1, t1, ropeb[:, 0, 0:n], ALU.mult)
                tt(t2, sw, ropeb[:, 1, 0:n], ALU.mult)
                tt(t1, t1, t2, ALU.add)
                tt(out, t1, rs, ALU.mult)

            for (cols, n) in blocks:
                latent = cols < SEQ
                mi = 0 if latent else 1
                prenorm(cols, n, A_mix, B_mix, mi, hTb)
                if latent:
                    P.dma('sp', ropeb[:, :, 0:n], rope_in[:, :, cols:cols + n], s_rope)
                    P.seal(s_rope)
                if STOP == "pre":
                    return
                if latent or need_ctx:
                    for c in range(KC):
                        if STOP == "q1" and c == 1:
                            return
                        raw = psA.next()[:, 0:n]
                        for kc in range(KC):
                            mm(raw, wqp[c][:, kc, :], hTb[:, kc, 0:n], kc == 0, kc == KC - 1)
                        qk_norm_rope(raw, n, gq, latent, qT[:, c, cols:cols + n])
                for c2 in range(2):
                    raw = psA.next()[:, 0:n]
                    for kc in range(KC):
                        mm(raw, wkp[:, c2, kc, :], hTb[:, kc, 0:n], kc == 0, kc == KC - 1)
                    qk_norm_rope(raw, n, gk, latent, kT[:, c2, cols:cols + n])
                if STOP == "k":
                    return
                for ti in range(n // 128):
                    tile_i = cols // 128 + ti
                    vps = psA.next()[:, 0:256]
                    for kc in range(KC):
                        mm(vps, hTb[:, kc, ti * 128:(ti + 1) * 128], wvp[:, kc, :], kc == 0, kc == KC - 1)
                    act(Vt[:, tile_i, :, 0:64], vps.rearrange("p (a b) -> p a b", b=64), AF.Copy)
                if STOP == "proj1":
                    return
            if STOP == "proj":
                return
            wop = [r13_load(wo_in[j * 8 + mo]) for mo in range(8)]
            qblocks = [(0, 512, list(range(18))), (512, 512, list(range(18))), (1024, 512, list(range(18))),
                       (1536, 512, list(range(18)))]
            if need_ctx:
                qblocks.append((2048, 256, [16, 17]))
            for c in range(KC):
                for hh in range(2):
                    kc2 = c // 4
                    jj = 2 * kc2 + hh
                    pr = slice(64 * hh, 64 * hh + 64)
                    for (qc, n, tiles) in qblocks:
                        O = psB.next()[:, 0:n]
                        S_list = {}

                        def issue_S(idx):
                            kt = tiles[idx]
                            S = psA.next()[:, 0:n]
                            mm(S, kT[pr, kc2, kt * 128:(kt + 1) * 128], qT[pr, c, qc:qc + n], True, True)
                            S_list[idx] = S
                        LA = 3
                        for idx in range(min(LA, len(tiles))):
                            issue_S(idx)
                        for idx, kt in enumerate(tiles):
                            PT = PT_r.next()[:, 0:n]
                            act(PT, S_list.pop(idx), AF.Exp, scale=0.125)
                            if idx + LA < len(tiles):
                                issue_S(idx + LA)
                            mm(O, Vt[:, kt, jj, :], PT, idx == 0, idx == len(tiles) - 1)
                        ssum = ssum_r.next()
                        act(ssum[0:64, 0:n], O[64:128, 0:n], AF.Copy)
                        recip(ssum[0:64, 0:n], ssum[0:64, 0:n])
                        tt(qT[pr, c, qc:qc + n], O[0:64, 0:n], ssum[0:64, 0:n], ALU.mult)
                        if STOP == "score1":
                            return
            if STOP == "score":
                return
            oblocks = [(i * 256, 256) for i in range(8)] + ([(2048, 256)] if need_ctx else [])
            for (cols, n) in oblocks:
                mi = 0 if cols < SEQ else 1
                ss = psB.next()[:, 0:n]
                for mo in range(8):
                    y = psA.next()[:, 0:n]
                    for kc in range(KC):
                        mm(y, wop[mo][:, kc, :], qT[:, kc, cols:cols + n], kc == 0, kc == KC - 1)
                    evac_y(y, ybuf[:, mo, :], ss, n, mo == 0, mo == 7)
                postnorm_residual(cols, n, ybuf, G_mix, mi, ss)

        for L in range(n_layers):
            need_ctx = L < DEPTH - 1
            modulation(L)
            if dbg_stage == ("mod", L):
                break
            if L % 2 == 0:
                conv_layer(L, L // 2)
            else:
                attn_layer(L, L // 2, need_ctx)
            if dbg_stage == ("mix", L):
                break
            sbs = [[(0, 384), (384, 384)], [(768, 384), (1152, 384)]]
            sbs.append([(1536, 512), (2048, 256)] if need_ctx else [(1536, 512)])
            ffn(L, sbs)

        s_o = P.slot()
        for c in range(KC):
            P.dma('sp', yT_out[c * 128:(c + 1) * 128, :], xT[:, c, 0:OUTC], s_o)
        P.seal(s_o)
        P.emit(final_slots=[s_o])
    return nc, P


def _vecT(v):
    v = np.asarray(v, np.float32)
    lead = v.shape[:-1]
    r = v.reshape(lead + (v.shape[-1] // 128, 128))
    r = np.moveaxis(r, -1, 0)
    return np.ascontiguousarray(r)


def _pieces(W):
    K, N = W.shape
    r = W.reshape(K // 128, 128, N // 128, 128).transpose(2, 1, 0, 3)
    return np.ascontiguousarray(r.reshape(N // 128, 128, (K // 128) * 128))


def _head_perm():
    perm = np.zeros(1024, np.int64)
    for c in range(8):
        for hh in range(2):
            head = (c if c < 4 else c + 4) + 4 * hh
            perm[c * 128 + hh * 64:c * 128 + hh * 64 + 64] = head * 64 + np.arange(64)
    return perm


def _rope_tables():
    pairs = 16
    freqs = (np.float32(10000.0) ** (-np.arange(pairs, dtype=np.float32) / np.float32(pairs))).astype(np.float32)
    t = np.arange(SEQ)
    row = (t // 64).astype(np.float32)
    col = (t % 64).astype(np.float32)
    ang = np.concatenate([row[:, None] * freqs[None, :], col[:, None] * freqs[None, :]], axis=-1).astype(np.float32)
    cos = np.cos(ang).astype(np.float32)
    sin = np.sin(ang).astype(np.float32)
    out = np.zeros((128, 2, SEQ), np.float32)
    for p in range(128):
        d = p % 64
        i = d // 2
        out[p, 0, :] = cos[:, i]
        out[p, 1, :] = -sin[:, i] if d % 2 == 0 else sin[:, i]
    return out


def _const_mats():
    m = np.zeros((128, 4, 128), np.float32)
    m[:, 0, :] = np.eye(128, dtype=np.float32)
    m[:, 1, :] = 1.0
    m[0:64, 2, 0:64] = 1.0
    m[64:128, 2, 64:128] = 1.0
    for k in range(128):
        m[k, 3, k ^ 1] = 1.0
    return np.ascontiguousarray(m.reshape(128, 512))


def prepare_inputs(x, c, ctx, c_ctx, w_mod, b_mod, norm_g,
                   conv_w_pw1, conv_b_pw1, conv_w_dw, conv_b_dw, conv_ln_g, conv_ln_b, conv_w_pw2, conv_b_pw2,
                   attn_wq, attn_wk, attn_wv, attn_wo, attn_q_g, attn_k_g, ffn_w1, ffn_w3, ffn_w2):
    f = lambda a: np.asarray(a, np.float32)
    x, c, ctx, c_ctx = f(x), f(c), f(ctx), f(c_ctx)
    shared = {}
    shared["mats"] = _const_mats()
    shared["rope"] = _rope_tables()
    shared["wmod"] = np.concatenate([_pieces(f(w_mod[l])) for l in range(DEPTH)], axis=0)
    shared["pw1"] = np.concatenate([_pieces(f(conv_w_pw1[j])) for j in range(2)], axis=0)
    shared["pw2"] = np.concatenate([_pieces(f(conv_w_pw2[j])) for j in range(2)], axis=0)
    perm = _head_perm()
    shared["wq"] = np.concatenate([_pieces(f(attn_wq[j])[:, perm]) for j in range(2)], axis=0)
    shared["wk"] = np.concatenate([_pieces(f(attn_wk[j])) for j in range(2)], axis=0)
    shared["wv"] = np.stack([np.ascontiguousarray(f(attn_wv[j]).reshape(8, 128, 256).transpose(1, 0, 2).reshape(128, 2048))
                             for j in range(2)], axis=0)
    shared["wo"] = np.concatenate([_pieces(f(attn_wo[j])[perm, :]) for j in range(2)], axis=0)
    shared["w1"] = np.concatenate([_pieces(f(ffn_w1[l])) for l in range(DEPTH)], axis=0)
    shared["w3"] = np.concatenate([_pieces(f(ffn_w3[l])) for l in range(DEPTH)], axis=0)
    shared["w2"] = np.concatenate([_pieces(f(ffn_w2[l])) for l in range(DEPTH)], axis=0)
    vec_common = np.zeros((128, NV), np.float32)
    vec_common[:, V_BMOD:V_BMOD + 192] = _vecT(f(b_mod).reshape(DEPTH, 6144)).reshape(128, 192)
    vec_common[:, V_NORMG:V_NORMG + 128] = _vecT(f(norm_g)).reshape(128, 128)
    for j in range(2):
        o = V_CONV + j * CONV_SZ
        vec_common[:, o:o + 16] = _vecT(f(conv_b_pw1[j]))
        vec_common[:, o + 16:o + 264] = _vecT(f(conv_w_dw[j])).transpose(0, 2, 1).reshape(128, 248)
        vec_common[:, o + 264:o + 272] = _vecT(f(conv_b_dw[j]))
        vec_common[:, o + 272:o + 280] = _vecT(f(conv_ln_g[j]))
        vec_common[:, o + 280:o + 288] = _vecT(f(conv_ln_b[j]))
        vec_common[:, o + 288:o + 296] = _vecT(f(conv_b_pw2[j]))
        vec_common[:, V_ATTN + 2 * j] = np.tile(f(attn_q_g[j]), 2)
        vec_common[:, V_ATTN + 2 * j + 1] = np.tile(f(attn_k_g[j]), 2)
    in_maps = []
    ccT = _vecT(c_ctx)
    for b in range(NCORES):
        m = dict(shared)
        m["xT"] = np.ascontiguousarray(np.concatenate([x[b].T, ctx[b].T], axis=1))
        v = vec_common.copy()
        cb = _vecT(c[b])
        v[:, V_CT:V_CT + 16] = np.stack([cb, ccT], axis=-1).reshape(128, 16)
        m["vecs"] = v
        in_maps.append(m)
    return in_maps


_CACHE = {}


def kernel(**inputs):
    in_maps = prepare_inputs(**inputs)
    if "nc" not in _CACHE:
        _CACHE["nc"] = build_program()[0]
    nc = _CACHE["nc"]
    res = run_bass_kernel_spmd(nc, in_maps, core_ids=list(range(NCORES)))
    out = np.stack([np.ascontiguousarray(r["yT"].T) for r in res.results], axis=0)
    return out.astype(np.float32)
```

```python
import numpy as np
import concourse.bass as bass
import concourse.mybir as mybir
from concourse.bass_utils import run_bass_kernel_spmd

F32 = mybir.dt.float32
BF16 = mybir.dt.bfloat16
U8 = mybir.dt.uint8
AF = mybir.ActivationFunctionType
ALU = mybir.AluOpType

_DTS = {F32: 4, BF16: 2, U8: 1}


def dtsize(dt):
    return _DTS[dt]


class _Op:
    __slots__ = ("eng", "fn", "slot", "deps", "signal", "tick", "idx")


class _Slot:
    __slots__ = ("sem", "count", "pending")


class Prog:
    ENGS = ("pe", "act", "dve", "pool", "sp")

    def __init__(self, nc):
        self.nc = nc
        self.ops = []
        self.recs = {}
        self.slots = []

    def slot(self):
        s = _Slot()
        s.sem = None
        s.count = 0
        s.pending = []
        self.slots.append(s)
        return s

    @staticmethod
    def _region(ap):
        t = ap.tensor
        tn = type(t).__name__
        if tn.startswith("DRam"):
            return None
        a = ap.ap
        es = _DTS[ap.dtype]
        pstride, pn = a[0]
        off = ap.offset
        p0 = off // pstride
        f0 = off - p0 * pstride
        ext = 1
        for s, c in a[1:]:
            ext += (c - 1) * abs(s)
        return (t.name, f0 * es, (f0 + ext) * es, p0, p0 + pn)

    def add(self, eng, fn, reads=(), writes=(), slot=None):
        op = _Op()
        op.eng = eng
        op.fn = fn
        op.slot = slot
        op.signal = False
        op.tick = 0
        op.idx = len(self.ops)
        deps = {}
        rregs = [r for r in (self._region(a) for a in reads) if r is not None]
        wregs = [r for r in (self._region(a) for a in writes) if r is not None]
        for (name, lo, hi, plo, phi) in rregs:
            for r in self.recs.get(name, ()):
                if r[5] and r[0] < hi and lo < r[1] and r[2] < phi and plo < r[3]:
                    deps[r[4]] = True
        for (name, lo, hi, plo, phi) in wregs:
            for r in self.recs.get(name, ()):
                if r[0] < hi and lo < r[1] and r[2] < phi and plo < r[3]:
                    if r[4] not in deps:
                        deps[r[4]] = False
        deps.pop(op.idx, None)
        op.deps = deps
        self.ops.append(op)
        for (name, lo, hi, plo, phi) in wregs:
            lst = self.recs.setdefault(name, [])
            lst[:] = [r for r in lst if not (lo <= r[0] and r[1] <= hi and plo <= r[2] and r[3] <= phi)]
            lst.append([lo, hi, plo, phi, op.idx, True])
        for (name, lo, hi, plo, phi) in rregs:
            lst = self.recs.setdefault(name, [])
            if slot is None:
                lst[:] = [r for r in lst if not ((not r[5]) and self.ops[r[4]].eng == eng and self.ops[r[4]].slot is None
                                                  and lo <= r[0] and r[1] <= hi and plo <= r[2] and r[3] <= phi)]
            lst.append([lo, hi, plo, phi, op.idx, False])
        return op

    def dma(self, eng, out, in_, slot):
        op = self.add(eng, lambda e: e.dma_start(out=out, in_=in_), reads=[in_], writes=[out], slot=slot)
        slot.count += 16
        slot.pending.append(op)
        return op

    def seal(self, slot):
        for op in slot.pending:
            op.tick = slot.count
        slot.pending = []

    def emit(self, final_slots=()):
        nc = self.nc
        ops = self.ops
        for s in self.slots:
            assert not s.pending, "unsealed DMA slot"
        for op in ops:
            best = {}
            for d, raw in op.deps.items():
                p = ops[d]
                if p.slot is not None:
                    key = ("d", id(p.slot))
                else:
                    if p.eng == op.eng and op.slot is None:
                        if p.eng == "pe" or not raw:
                            continue
                    key = ("e", p.eng)
                if key not in best or best[key].idx < p.idx:
                    best[key] = p
            op.deps = best
            for key, p in best.items():
                if p.slot is None:
                    p.signal = True
        cnt = {e: 0 for e in self.ENGS}
        for op in ops:
            if op.slot is None and op.signal:
                cnt[op.eng] += 1
                op.tick = cnt[op.eng]
        import contextlib
        with contextlib.ExitStack() as st:
            esem = {e: st.enter_context(nc.semaphore("S_" + e)) for e in self.ENGS}
            for i, s in enumerate(self.slots):
                s.sem = st.enter_context(nc.semaphore("D%d" % i))
            block = st.enter_context(nc.Block())
            self.n_waits = 0

            def stream(engname, e):
                waited = {}
                for op in ops:
                    if op.eng != engname:
                        continue
                    need = {}
                    for key, p in op.deps.items():
                        sem = p.slot.sem if p.slot is not None else esem[p.eng]
                        need[key] = (sem, p.tick)
                    todo = []
                    for key, (sem, val) in need.items():
                        if val > waited.get(key, 0):
                            todo.append((sem, val))
                            waited[key] = val
                    for (sem, val) in todo[1:]:
                        e.wait_ge(sem, val)
                        self.n_waits += 1
                    ins = op.fn(e)
                    if todo:
                        ins._wait_ge(todo[0][0], todo[0][1])
                    if op.slot is not None:
                        ins.then_inc(op.slot.sem, 16)
                    elif op.signal:
                        ins.then_inc(esem[op.eng], 1)
                if engname == "sp":
                    for s in final_slots:
                        e.wait_ge(s.sem, s.count)

            @block.tensor
            def _(e):
                stream("pe", e)

            @block.scalar
            def _(e):
                stream("act", e)

            @block.vector
            def _(e):
                stream("dve", e)

            @block.gpsimd
            def _(e):
                stream("pool", e)

            @block.sync
            def _(e):
                stream("sp", e)


D = 1024
KC = 8
SEQ = 2048
CTXL = 256
T = SEQ + CTXL
DFF = 2816
MFF = 22
DEPTH = 4
EPS = 1e-6
NCORES = 8

V_BMOD = 0
V_NORMG = V_BMOD + 192
V_CONV = V_NORMG + 128
CONV_SZ = 296
V_ATTN = V_CONV + 2 * CONV_SZ
V_CT = V_ATTN + 4
NV = V_CT + 16

XT_OFF = 0
XT_SZ = KC * T * 4
R13_OFF = XT_OFF + XT_SZ
R13_SLOT = 2048
R13_N = 8
R2_OFF = R13_OFF + R13_SLOT * R13_N
R2_SLOT = 5632
R2_N = 2
CONST_OFF = R2_OFF + R2_SLOT * R2_N
CONST_SZ = 7168
AR_OFF = CONST_OFF + CONST_SZ
TOTAL = 210944
SCR_SZ = 24576
SCR_OFF = TOTAL - SCR_SZ
PH_SZ = SCR_OFF - AR_OFF


class _Rot:
    def __init__(self, items):
        self.items = items
        self.i = 0

    def next(self):
        v = self.items[self.i % len(self.items)]
        self.i += 1
        return v


def build_program(n_layers=DEPTH, dbg_stage=None, full_out=False):
    nc = bass.Bass("TRN2", target_bir_lowering=False)

    def dram(name, shape, kind="ExternalInput"):
        return nc.dram_tensor(name, shape, F32, kind=kind).ap()

    xT_in = dram("xT", [D, T])
    vecs_in = dram("vecs", [128, NV])
    mats_in = dram("mats", [128, 4 * 128])
    rope_in = dram("rope", [128, 2, SEQ])
    wmod_in = dram("wmod", [DEPTH * 24, 128, 2048])
    pw1_in = dram("pw1", [2 * 8, 128, 2048])
    pw2_in = dram("pw2", [2 * 8, 128, 1024])
    wq_in = dram("wq", [2 * 8, 128, 1024])
    wk_in = dram("wk", [2 * 2, 128, 1024])
    wv_in = dram("wv", [2, 128, 2048])
    wo_in = dram("wo", [2 * 8, 128, 1024])
    w13_in = dram("w13", [DEPTH * MFF, 128, 2048])
    w2_in = dram("w2", [DEPTH * 8, 128, DFF])
    OUTC = T if full_out else SEQ
    yT_out = dram("yT", [D, OUTC], kind="ExternalOutput")

    P = Prog(nc)
    import contextlib
    st = contextlib.ExitStack()
    with st:
        ar = st.enter_context(nc.sbuf_tensor("ar", [128, TOTAL], U8))
        banks = [st.enter_context(nc.psum_tensor("pb%d" % i, [128, 512], F32)) for i in range(8)]
        psA = _Rot(banks[0:6])
        psB = _Rot(banks[6:8])

        def view(off, shape, dt):
            n = int(np.prod(shape))
            assert off % 4 == 0 and off + n * dtsize(dt) <= TOTAL, (off, shape)
            v = ar[:, off:off + n * dtsize(dt)].bitcast(dt)
            if len(shape) == 2:
                v = v.rearrange("p (a b) -> p a b", b=shape[1])
            elif len(shape) == 3:
                v = v.rearrange("p (a b c) -> p a b c", b=shape[1], c=shape[2])
            return v

        xT = view(XT_OFF, [KC, T], F32)
        r13 = [view(R13_OFF + i * R13_SLOT, [KC, 128], BF16) for i in range(R13_N)]
        r13_slots = [P.slot() for _ in range(R13_N)]
        r13_i = [0]
        r2_slots = [P.slot() for _ in range(R2_N)]
        r2_i = [0]
        co = [CONST_OFF]

        def calloc(shape, dt):
            n = int(np.prod(shape)) * dtsize(dt)
            n = (n + 3) // 4 * 4
            v = view(co[0], shape, dt)
            co[0] += n
            assert co[0] <= CONST_OFF + CONST_SZ
            return v

        vecs = calloc([NV], F32)
        mats = calloc([4, 128], BF16)
        ident, ones_m, blk64, swapm = mats[:, 0, :], mats[:, 1, :], mats[:, 2, :], mats[:, 3, :]
        scT = calloc([KC, 2], BF16)
        mod_sb = calloc([48, 2], F32)
        A_mix = calloc([KC, 2], F32)
        G_mix = calloc([KC, 2], F32)
        A_ffn = calloc([KC, 2], F32)
        G_ffn = calloc([KC, 2], F32)
        epsT = calloc([1], F32)

        xsq = view(SCR_OFF, [KC, 512], BF16)
        rstd_r = _Rot([view(SCR_OFF + 8192 + i * 2048, [512], F32) for i in range(2)])
        nt_r = _Rot([view(SCR_OFF + 12288 + i * 2048, [512], F32) for i in range(3)])
        ysq_r = _Rot([view(SCR_OFF + 18432 + i * 1024, [512], BF16) for i in range(2)])
        stmp_r = _Rot([view(SCR_OFF + 20480 + i * 2048, [512], F32) for i in range(2)])

        def mm(out, lhsT, rhs, start, stop):
            P.add('pe', lambda e: e.matmul(out, lhsT, rhs, start=start, stop=stop), reads=[lhsT, rhs], writes=[out])

        def act(out, in_, func, bias=None, scale=None):
            kw = {}
            reads = [in_]
            if bias is not None:
                kw['bias'] = bias
                if not isinstance(bias, float):
                    reads.append(bias)
            if scale is not None:
                kw['scale'] = scale
                if not isinstance(scale, float):
                    reads.append(scale)
            P.add('act', lambda e: e.activation(out, in_, func, **kw), reads=reads, writes=[out])

        def tt(out, in0, in1, op, eng='dve'):
            P.add(eng, lambda e: e.tensor_tensor(out, in0, in1, op), reads=[in0, in1], writes=[out])

        def stt(out, in0, scalar, in1, op0, op1, eng='dve'):
            reads = [in0, in1] + ([] if isinstance(scalar, float) else [scalar])
            P.add(eng, lambda e: e.scalar_tensor_tensor(out, in0, scalar, in1, op0, op1), reads=reads, writes=[out])

        def tsc(out, in0, s1, op0, s2=None, op1=None, eng='dve'):
            reads = [in0] + [s for s in (s1, s2) if s is not None and not isinstance(s, float)]
            if op1 is None:
                P.add(eng, lambda e: e.tensor_scalar(out, in0, s1, None, op0), reads=reads, writes=[out])
            else:
                P.add(eng, lambda e: e.tensor_scalar(out, in0, s1, s2, op0, op1), reads=reads, writes=[out])

        def recip(out, in_):
            P.add('dve', lambda e: e.reciprocal(out, in_), reads=[in_], writes=[out])

        def memset(ap, val, eng='dve'):
            P.add(eng, lambda e: e.memset(ap, val), writes=[ap])

        def bcast(ap, pattern):
            a = ap.ap
            return bass.AP(ap.tensor, ap.offset, [list(a[0])] + [list(x) for x in pattern])

        def r13_load(src):
            i = r13_i[0] % R13_N
            r13_i[0] += 1
            P.dma('pool', r13[i], src.rearrange("p (a b) -> p a b", b=128), r13_slots[i])
            P.seal(r13_slots[i])
            return r13[i]

        def r13_load2(src):
            if r13_i[0] % 2:
                r13_i[0] += 1
            i = r13_i[0] % R13_N
            r13_i[0] += 2
            v = view(R13_OFF + i * R13_SLOT, [2, KC, 128], BF16)
            P.dma('pool', v, src.rearrange("p (a k f) -> p a k f", a=2, f=128), r13_slots[i])
            P.seal(r13_slots[i])
            return v[:, 0], v[:, 1]

        def r2_load(src, shape):
            i = r2_i[0] % R2_N
            r2_i[0] += 1
            v = view(R2_OFF + i * R2_SLOT, shape, BF16)
            if len(src.shape) == 2:
                src = src.rearrange("p (a b) -> p a b", b=shape[1])
            P.dma('pool', v, src, r2_slots[i])
            P.seal(r2_slots[i])
            return v

        s_x = P.slot()
        for c in range(KC):
            P.dma('sp', xT[:, c, :], xT_in[c * 128:(c + 1) * 128, :], s_x)
        P.seal(s_x)
        s_v = P.slot()
        P.dma('sp', vecs, vecs_in, s_v)
        P.seal(s_v)
        s_m = P.slot()
        P.dma('pool', mats, mats_in.rearrange("p (a b) -> p a b", b=128), s_m)
        P.seal(s_m)
        memset(epsT, EPS)
        cT = vecs[:, V_CT:V_CT + 16].rearrange("p (a b) -> p a b", b=2)
        act(scT, cT, AF.Silu)

        def normg(L, i):
            o = V_NORMG + (L * 4 + i) * 8
            return vecs[:, o:o + 8]

        def rstd_from(ss, n, scale):
            rs = rstd_r.next()[:, 0:n]
            act(rs, ss, AF.Sqrt, bias=epsT[:, 0:1], scale=scale)
            recip(rs, rs)
            return rs

        def prenorm(cols, n, A, B, mi, out):
            act(xsq[:, :, 0:n], xT[:, :, cols:cols + n], AF.Square)
            ss = psB.next()[:, 0:n]
            for c in range(KC):
                mm(ss, ones_m, xsq[:, c, 0:n], c == 0, c == KC - 1)
            rs = rstd_from(ss, n, 1.0 / D)
            for c in range(KC):
                t = nt_r.next()[:, 0:n]
                stt(t, xT[:, c, cols:cols + n], A[:, c, mi:mi + 1], rs, ALU.mult, ALU.mult)
                act(out[:, c, 0:n], t, AF.Identity, bias=B[:, c, mi:mi + 1])

        def evac_y(y_ps, ybuf_c, ss, n, first, last, bias=None):
            if bias is None:
                act(ybuf_c, y_ps, AF.Copy)
                q = ysq_r.next()[:, 0:n]
                act(q, y_ps, AF.Square)
            else:
                act(ybuf_c, y_ps, AF.Identity, bias=bias)
                q = ysq_r.next()[:, 0:n]
                act(q, y_ps, AF.Square, bias=bias)
            mm(ss, ones_m, q, first, last)

        def postnorm_residual(cols, n, ybuf, G, mi, ss):
            rs = rstd_from(ss, n, 1.0 / D)
            for c in range(KC):
                t = nt_r.next()[:, 0:n]
                stt(t, ybuf[:, c, 0:n], G[:, c, mi:mi + 1], rs, ALU.mult, ALU.mult)
                tt(xT[:, c, cols:cols + n], xT[:, c, cols:cols + n], t, ALU.add)

        def modulation(L):
            mps = psB.next()[:, 0:96]
            for j2 in range(24):
                wpair = r13_load2(wmod_in[L * 24 + j2])
                for jj in range(2):
                    j = 2 * j2 + jj
                    w = wpair[jj]
                    for kc in range(KC):
                        mm(mps[:, 2 * j:2 * j + 2], w[:, kc, :], scT[:, kc, :], kc == 0, kc == KC - 1)
            bm = vecs[:, V_BMOD + L * 48:V_BMOD + (L + 1) * 48]
            tt(mod_sb, mps.rearrange("p (a b) -> p a b", b=2), bcast(bm, [[1, 48], [0, 2]]), ALU.add)
            g0 = bcast(normg(L, 0), [[1, 8], [0, 2]])
            g1 = bcast(normg(L, 1), [[1, 8], [0, 2]])
            g2 = bcast(normg(L, 2), [[1, 8], [0, 2]])
            g3 = bcast(normg(L, 3), [[1, 8], [0, 2]])
            stt(A_mix, mod_sb[:, 8:16, :], 1.0, g0, ALU.add, ALU.mult)
            tt(G_mix, mod_sb[:, 16:24, :], g1, ALU.mult)
            stt(A_ffn, mod_sb[:, 32:40, :], 1.0, g2, ALU.add, ALU.mult)
            tt(G_ffn, mod_sb[:, 40:48, :], g3, ALU.mult)

        B_mix = mod_sb[:, 0:8, :]
        B_ffn = mod_sb[:, 24:32, :]

        def ffn(L, sbs):
            g = view(AR_OFF, [MFF, 768], BF16)
            hT = view(AR_OFF + 33792, [KC, 768], BF16)
            ybuf = view(AR_OFF + 46080, [KC, 768], F32)
            for sb in sbs:
                sb0 = sb[0][0]
                for (cols, n) in sb:
                    mi = 0 if cols < SEQ else 1
                    prenorm(cols, n, A_ffn, B_ffn, mi, hT[:, :, cols - sb0:cols - sb0 + n])
                AH = 4
                pend = []
                for m in range(min(AH, MFF)):
                    pend.append(r13_load2(w13_in[L * MFF + m]))
                w2v = [None] * 8
                w2v[0] = r2_load(w2_in[L * 8 + 0], [MFF, 128])
                for m in range(MFF):
                    w1v, w3v = pend.pop(0)
                    for (cols, n) in sb:
                        lc = cols - sb0
                        h1 = psA.next()[:, 0:n]
                        h3 = psA.next()[:, 0:n]
                        for kc in range(KC):
                            mm(h1, w1v[:, kc, :], hT[:, kc, lc:lc + n], kc == 0, kc == KC - 1)
                        for kc in range(KC):
                            mm(h3, w3v[:, kc, :], hT[:, kc, lc:lc + n], kc == 0, kc == KC - 1)
                        s = stmp_r.next()[:, 0:n]
                        act(s, h1, AF.Silu)
                        tt(g[:, m, lc:lc + n], h3, s, ALU.mult)
                    if m + AH < MFF:
                        pend.append(r13_load2(w13_in[L * MFF + m + AH]))
                sss = [psB.next() for _ in sb]
                w2v[1] = r2_load(w2_in[L * 8 + 1], [MFF, 128])
                for mo in range(8):
                    for bi, (cols, n) in enumerate(sb):
                        lc = cols - sb0
                        y = psA.next()[:, 0:n]
                        for kc in range(MFF):
                            mm(y, w2v[mo][:, kc, :], g[:, kc, lc:lc + n], kc == 0, kc == MFF - 1)
                        evac_y(y, ybuf[:, mo, lc:lc + n], sss[bi][:, 0:n], n, mo == 0, mo == 7)
                    if mo + 2 < 8:
                        w2v[mo + 2] = r2_load(w2_in[L * 8 + mo + 2], [MFF, 128])
                for bi, (cols, n) in enumerate(sb):
                    lc = cols - sb0
                    mi = 0 if cols < SEQ else 1
                    postnorm_residual(cols, n, ybuf[:, :, lc:lc + n], G_ffn, mi, sss[bi][:, 0:n])

        def conv_layer(L, j):
            vb = V_CONV + j * CONV_SZ
            b_pw1 = vecs[:, vb:vb + 16]
            wdw = vecs[:, vb + 16:vb + 16 + 248].rearrange("p (a b) -> p a b", b=31)
            b_dw = vecs[:, vb + 264:vb + 272]
            ln_g = vecs[:, vb + 272:vb + 280]
            ln_b = vecs[:, vb + 280:vb + 288]
            b_pw2 = vecs[:, vb + 288:vb + 296]
            BW = 813
            hTp = view(AR_OFF, [KC, 816], BF16)
            glu1 = [view(AR_OFF + 13056 + i * 1632, [816], BF16) for i in range(2)]
            diag = [view(AR_OFF + 16320 + i * 7936, [31, 128], BF16) for i in range(2)]
            vbuf = view(AR_OFF + 32192, [KC, 768], F32)
            zb = view(AR_OFF + 56768, [KC, 256], BF16)
            ybuf = view(AR_OFF + 60864, [KC, 256], F32)
            stat = [view(AR_OFF + 69056 + i * 1024, [256], F32) for i in range(4)]
            parts = [
                (None, [(0, 392, 15), (392, 391, 407)], [(15, 392), (407, 391)], [(0, 15)],
                 [(0, 384, 0, 0), (384, 384, 384, 384)]),
                ((753, 15, 0), [(768, 384, 15), (1152, 399, 399)], [(0, 399), (399, 399)], [],
                 [(768, 384, 0, 0), (1152, 384, 384, 384)]),
                ((1521, 15, 0), [(1536, 249, 15), (1785, 263, 264), (2048, 256, 542)], [(0, 264), (264, 263), (542, 256)],
                 [(527, 15), (798, 15)], [(1536, 512, 0, 0), (2048, 256, 527, 512)]),
            ]
            for pi, (halo, nblocks, pblocks, zeros, oblocks) in enumerate(parts):
                for (t0, n, bc0) in nblocks:
                    mi = 0 if t0 < SEQ else 1
                    prenorm(t0, n, A_mix, B_mix, mi, hTp[:, :, bc0:bc0 + n])
                for m in range(KC):
                    wa, wg = r13_load2(pw1_in[j * 8 + m])
                    gl = glu1[m % 2]
                    for (z0, zn) in zeros:
                        memset(gl[:, z0:z0 + zn], 0.0)
                    for (bc0, n) in pblocks:
                        a_ps = psA.next()[:, 0:n]
                        g_ps = psA.next()[:, 0:n]
                        for kc in range(KC):
                            mm(a_ps, wa[:, kc, :], hTp[:, kc, bc0:bc0 + n], kc == 0, kc == KC - 1)
                        for kc in range(KC):
                            mm(g_ps, wg[:, kc, :], hTp[:, kc, bc0:bc0 + n], kc == 0, kc == KC - 1)
                        sg = stmp_r.next()[:, 0:n]
                        act(sg, g_ps, AF.Sigmoid, bias=b_pw1[:, 8 + m:9 + m])
                        stt(gl[:, bc0:bc0 + n], a_ps, b_pw1[:, m:m + 1], sg, ALU.add, ALU.mult)
                    dg = diag[m % 2]
                    tt(dg, bcast(ident, [[0, 31], [1, 128]]), bcast(wdw[:, m, :], [[1, 31], [0, 128]]), ALU.mult)
                    for (t0, n, bt0, lc) in oblocks:
                        v_ps = psA.next()[:, 0:n]
                        for tap in range(31):
                            mm(v_ps, dg[:, tap, :], gl[:, bt0 + tap:bt0 + tap + n], tap == 0, tap == 30)
                        act(vbuf[:, m, lc:lc + n], v_ps, AF.Identity, bias=b_dw[:, m:m + 1])
                if pi + 1 < len(parts):
                    (t0, n, bc0) = parts[pi + 1][0]
                    prenorm(t0, n, A_mix, B_mix, 0, hTp[:, :, bc0:bc0 + n])
                w2p = [r13_load(pw2_in[j * 8 + mo]) for mo in range(8)]
                for (t0, n, bt0, lc0) in oblocks:
                    for sub in range(0, n, 256):
                        tk = t0 + sub
                        lc = lc0 + sub
                        nn = min(256, n - sub)
                        mi = 0 if tk < SEQ else 1
                        vs = xsq[:, :, 0:nn]
                        vq = xsq[:, :, 256:256 + nn]
                        act(vs, vbuf[:, :, lc:lc + nn], AF.Copy)
                        act(vq, vbuf[:, :, lc:lc + nn], AF.Square)
                        s1 = psB.next()[:, 0:nn]
                        s2 = psB.next()[:, 0:nn]
                        for c in range(KC):
                            mm(s1, ones_m, vs[:, c, :], c == 0, c == KC - 1)
                        for c in range(KC):
                            mm(s2, ones_m, vq[:, c, :], c == 0, c == KC - 1)
                        mean, msq, rs, nmr = stat[0][:, 0:nn], stat[1][:, 0:nn], stat[2][:, 0:nn], stat[3][:, 0:nn]
                        act(mean, s1, AF.Copy, scale=1.0 / D)
                        tt(msq, mean, mean, ALU.mult)
                        stt(msq, s2, 1.0 / D, msq, ALU.mult, ALU.subtract)
                        act(rs, msq, AF.Sqrt, bias=epsT[:, 0:1], scale=1.0)
                        recip(rs, rs)
                        stt(nmr, mean, -1.0, rs, ALU.mult, ALU.mult)
                        for c in range(KC):
                            t = nt_r.next()[:, 0:nn]
                            tt(t, vbuf[:, c, lc:lc + nn], rs, ALU.mult)
                            tt(t, t, nmr, ALU.add)
                            act(zb[:, c, 0:nn], t, AF.Silu, bias=ln_b[:, c:c + 1], scale=ln_g[:, c:c + 1])
                        ss = psB.next()[:, 0:nn]
                        for mo in range(8):
                            y = psA.next()[:, 0:nn]
                            for kc in range(KC):
                                mm(y, w2p[mo][:, kc, :], zb[:, kc, 0:nn], kc == 0, kc == KC - 1)
                            evac_y(y, ybuf[:, mo, 0:nn], ss, nn, mo == 0, mo == 7, bias=b_pw2[:, mo:mo + 1])
                        postnorm_residual(tk, nn, ybuf, G_mix, mi, ss)

        def attn_layer(L, j, need_ctx):
            gq = vecs[:, V_ATTN + 2 * j:V_ATTN + 2 * j + 1]
            gk = vecs[:, V_ATTN + 2 * j + 1:V_ATTN + 2 * j + 2]
            qT = view(AR_OFF, [KC, T], BF16)
            kT = view(AR_OFF + 36864, [2, T], BF16)
            Vt = view(AR_OFF + 46080, [18, 4, 128], BF16)
            hTb = view(AR_OFF + 64512, [KC, 512], BF16)
            ybuf = view(AR_OFF + 64512, [KC, 256], F32)
            ropeb = view(AR_OFF + 72704, [2, 512], F32)
            assert 76800 <= PH_SZ
            PT_r = _Rot([view(SCR_OFF + i * 1024, [512], BF16) for i in range(4)])
            ssum_r = _Rot([view(SCR_OFF + 8192 + i * 2048, [512], F32) for i in range(2)])
            qg_r = _Rot([view(SCR_OFF + 4096 + i * 1024, [512], BF16) for i in range(2)])
            s_rope = P.slot()
            wqp = [r13_load(wq_in[j * 8 + c]) for c in range(8)]
            wkp = r2_load(wk_in[j * 2:(j + 1) * 2].rearrange("a p (k f) -> p a k f", f=128), [2, KC, 128])
            wvp = r2_load(wv_in[j], [KC, 256])
            import os
            STOP = os.environ.get("ATTN_STOP", "")
            memset(Vt[:, :, :, 64:128], 1.0)
            if STOP == "memset":
                return
            blocks = [(0, 512), (512, 512), (1024, 512), (1536, 512), (2048, 256)]

            def qk_norm_rope(raw, n, gain, rope, out):
                sq = ysq_r.next()[:, 0:n]
                act(sq, raw, AF.Square)
                ss = psB.next()[:, 0:n]
                mm(ss, blk64, sq, True, True)
                if not rope:
                    rs = rstd_from(ss, n, 1.0 / 64)
                    stt(out, raw, gain, rs, ALU.mult, ALU.mult)
                    return
                t1 = nt_r.next()[:, 0:n]
                act(t1, raw, AF.Copy, scale=gain)
                qg = qg_r.next()[:, 0:n]
                P.add('dve', lambda e: e.tensor_copy(qg, t1), reads=[t1], writes=[qg])
                sw = psB.next()[:, 0:n]
                mm(sw, swapm, qg, True, True)
                rs = rstd_from(ss, n, 1.0 / 64)
                t2 = nt_r.next()[:, 0:n]
                tt(t1, t1, ropeb[:, 0, 0:n], ALU.mult)
                tt(t2, sw, ropeb[:, 1, 0:n], ALU.mult)
                tt(t1, t1, t2, ALU.add)
                tt(out, t1, rs, ALU.mult)

            for (cols, n) in blocks:
                latent = cols < SEQ
                mi = 0 if latent else 1
                prenorm(cols, n, A_mix, B_mix, mi, hTb)
                if latent:
                    P.dma('sp', ropeb[:, :, 0:n], rope_in[:, :, cols:cols + n], s_rope)
                    P.seal(s_rope)
                if STOP == "pre":
                    return
                if latent or need_ctx:
                    for c in range(KC):
                        if STOP == "q1" and c == 1:
                            return
                        raw = psA.next()[:, 0:n]
                        for kc in range(KC):
                            mm(raw, wqp[c][:, kc, :], hTb[:, kc, 0:n], kc == 0, kc == KC - 1)
                        qk_norm_rope(raw, n, gq, latent, qT[:, c, cols:cols + n])
                for c2 in range(2):
                    raw = psA.next()[:, 0:n]
                    for kc in range(KC):
                        mm(raw, wkp[:, c2, kc, :], hTb[:, kc, 0:n], kc == 0, kc == KC - 1)
                    qk_norm_rope(raw, n, gk, latent, kT[:, c2, cols:cols + n])
                if STOP == "k":
                    return
                for ti in range(n // 128):
                    tile_i = cols // 128 + ti
                    vps = psA.next()[:, 0:256]
                    for kc in range(KC):
                        mm(vps, hTb[:, kc, ti * 128:(ti + 1) * 128], wvp[:, kc, :], kc == 0, kc == KC - 1)
                    act(Vt[:, tile_i, :, 0:64], vps.rearrange("p (a b) -> p a b", b=64), AF.Copy)
                if STOP == "proj1":
                    return
            if STOP == "proj":
                return
            wop = [r13_load(wo_in[j * 8 + mo]) for mo in range(8)]
            qblocks = [(0, 512, list(range(18))), (512, 512, list(range(18))), (1024, 512, list(range(18))),
                       (1536, 512, list(range(18)))]
            if need_ctx:
                qblocks.append((2048, 256, [16, 17]))
            for c in range(KC):
                for hh in range(2):
                    kc2 = c // 4
                    jj = 2 * kc2 + hh
                    pr = slice(64 * hh, 64 * hh + 64)
                    for (qc, n, tiles) in qblocks:
                        O = psB.next()[:, 0:n]
                        S_list = {}

                        def issue_S(idx):
                            kt = tiles[idx]
                            S = psA.next()[:, 0:n]
                            mm(S, kT[pr, kc2, kt * 128:(kt + 1) * 128], qT[pr, c, qc:qc + n], True, True)
                            S_list[idx] = S
                        LA = 3
                        for idx in range(min(LA, len(tiles))):
                            issue_S(idx)
                        for idx, kt in enumerate(tiles):
                            PT = PT_r.next()[:, 0:n]
                            act(PT, S_list.pop(idx), AF.Exp, scale=0.125)
                            if idx + LA < len(tiles):
                                issue_S(idx + LA)
                            mm(O, Vt[:, kt, jj, :], PT, idx == 0, idx == len(tiles) - 1)
                        ssum = ssum_r.next()
                        act(ssum[0:64, 0:n], O[64:128, 0:n], AF.Copy)
                        recip(ssum[0:64, 0:n], ssum[0:64, 0:n])
                        tt(qT[pr, c, qc:qc + n], O[0:64, 0:n], ssum[0:64, 0:n], ALU.mult)
                        if STOP == "score1":
                            return
            if STOP == "score":
                return
            oblocks = [(i * 256, 256) for i in range(8)] + ([(2048, 256)] if need_ctx else [])
            for (cols, n) in oblocks:
                mi = 0 if cols < SEQ else 1
                ss = psB.next()[:, 0:n]
                for mo in range(8):
                    y = psA.next()[:, 0:n]
                    for kc in range(KC):
                        mm(y, wop[mo][:, kc, :], qT[:, kc, cols:cols + n], kc == 0, kc == KC - 1)
                    evac_y(y, ybuf[:, mo, :], ss, n, mo == 0, mo == 7)
                postnorm_residual(cols, n, ybuf, G_mix, mi, ss)

        for L in range(n_layers):
            need_ctx = L < DEPTH - 1
            modulation(L)
            if dbg_stage == ("mod", L):
                break
            if L % 2 == 0:
                conv_layer(L, L // 2)
            else:
                attn_layer(L, L // 2, need_ctx)
            if dbg_stage == ("mix", L):
                break
            sbs = [[(0, 384), (384, 384)], [(768, 384), (1152, 384)]]
            sbs.append([(1536, 512), (2048, 256)] if need_ctx else [(1536, 512)])
            ffn(L, sbs)

        s_o = P.slot()
        for c in range(KC):
            P.dma('sp', yT_out[c * 128:(c + 1) * 128, :], xT[:, c, 0:OUTC], s_o)
        P.seal(s_o)
        P.emit(final_slots=[s_o])
    return nc, P


def _vecT(v):
    v = np.asarray(v, np.float32)
    lead = v.shape[:-1]
    r = v.reshape(lead + (v.shape[-1] // 128, 128))
    r = np.moveaxis(r, -1, 0)
    return np.ascontiguousarray(r)


def _pieces(W):
    K, N = W.shape
    r = W.reshape(K // 128, 128, N // 128, 128).transpose(2, 1, 0, 3)
    return np.ascontiguousarray(r.reshape(N // 128, 128, (K // 128) * 128))


def _head_perm():
    perm = np.zeros(1024, np.int64)
    for c in range(8):
        for hh in range(2):
            head = (c if c < 4 else c + 4) + 4 * hh
            perm[c * 128 + hh * 64:c * 128 + hh * 64 + 64] = head * 64 + np.arange(64)
    return perm


def _rope_tables():
    pairs = 16
    freqs = (np.float32(10000.0) ** (-np.arange(pairs, dtype=np.float32) / np.float32(pairs))).astype(np.float32)
    t = np.arange(SEQ)
    row = (t // 64).astype(np.float32)
    col = (t % 64).astype(np.float32)
    ang = np.concatenate([row[:, None] * freqs[None, :], col[:, None] * freqs[None, :]], axis=-1).astype(np.float32)
    cos = np.cos(ang).astype(np.float32)
    sin = np.sin(ang).astype(np.float32)
    out = np.zeros((128, 2, SEQ), np.float32)
    for p in range(128):
        d = p % 64
        i = d // 2
        out[p, 0, :] = cos[:, i]
        out[p, 1, :] = -sin[:, i] if d % 2 == 0 else sin[:, i]
    return out


def _const_mats():
    m = np.zeros((128, 4, 128), np.float32)
    m[:, 0, :] = np.eye(128, dtype=np.float32)
    m[:, 1, :] = 1.0
    m[0:64, 2, 0:64] = 1.0
    m[64:128, 2, 64:128] = 1.0
    for k in range(128):
        m[k, 3, k ^ 1] = 1.0
    return np.ascontiguousarray(m.reshape(128, 512))


def prepare_inputs(x, c, ctx, c_ctx, w_mod, b_mod, norm_g,
                   conv_w_pw1, conv_b_pw1, conv_w_dw, conv_b_dw, conv_ln_g, conv_ln_b, conv_w_pw2, conv_b_pw2,
                   attn_wq, attn_wk, attn_wv, attn_wo, attn_q_g, attn_k_g, ffn_w1, ffn_w3, ffn_w2):
    f = lambda a: np.asarray(a, np.float32)
    x, c, ctx, c_ctx = f(x), f(c), f(ctx), f(c_ctx)
    shared = {}
    shared["mats"] = _const_mats()
    shared["rope"] = _rope_tables()
    def _pair(a, b):
        return np.ascontiguousarray(np.concatenate([a, b], axis=-1))
    wm = np.concatenate([_pieces(f(w_mod[l])) for l in range(DEPTH)], axis=0)
    shared["wmod"] = _pair(wm[0::2], wm[1::2])
    p1 = [_pieces(f(conv_w_pw1[j])) for j in range(2)]
    shared["pw1"] = np.concatenate([_pair(p[0:8], p[8:16]) for p in p1], axis=0)
    shared["pw2"] = np.concatenate([_pieces(f(conv_w_pw2[j])) for j in range(2)], axis=0)
    perm = _head_perm()
    shared["wq"] = np.concatenate([_pieces(f(attn_wq[j])[:, perm]) for j in range(2)], axis=0)
    shared["wk"] = np.concatenate([_pieces(f(attn_wk[j])) for j in range(2)], axis=0)
    shared["wv"] = np.stack([np.ascontiguousarray(f(attn_wv[j]).reshape(8, 128, 256).transpose(1, 0, 2).reshape(128, 2048))
                             for j in range(2)], axis=0)
    shared["wo"] = np.concatenate([_pieces(f(attn_wo[j])[perm, :]) for j in range(2)], axis=0)
    shared["w13"] = _pair(np.concatenate([_pieces(f(ffn_w1[l])) for l in range(DEPTH)], axis=0),
                          np.concatenate([_pieces(f(ffn_w3[l])) for l in range(DEPTH)], axis=0))
    shared["w2"] = np.concatenate([_pieces(f(ffn_w2[l])) for l in range(DEPTH)], axis=0)
    vec_common = np.zeros((128, NV), np.float32)
    vec_common[:, V_BMOD:V_BMOD + 192] = _vecT(f(b_mod).reshape(DEPTH, 6144)).reshape(128, 192)
    vec_common[:, V_NORMG:V_NORMG + 128] = _vecT(f(norm_g)).reshape(128, 128)
    for j in range(2):
        o = V_CONV + j * CONV_SZ
        vec_common[:, o:o + 16] = _vecT(f(conv_b_pw1[j]))
        vec_common[:, o + 16:o + 264] = _vecT(f(conv_w_dw[j])).transpose(0, 2, 1).reshape(128, 248)
        vec_common[:, o + 264:o + 272] = _vecT(f(conv_b_dw[j]))
        vec_common[:, o + 272:o + 280] = _vecT(f(conv_ln_g[j]))
        vec_common[:, o + 280:o + 288] = _vecT(f(conv_ln_b[j]))
        vec_common[:, o + 288:o + 296] = _vecT(f(conv_b_pw2[j]))
        vec_common[:, V_ATTN + 2 * j] = np.tile(f(attn_q_g[j]), 2)
        vec_common[:, V_ATTN + 2 * j + 1] = np.tile(f(attn_k_g[j]), 2)
    in_maps = []
    ccT = _vecT(c_ctx)
    for b in range(NCORES):
        m = dict(shared)
        m["xT"] = np.ascontiguousarray(np.concatenate([x[b].T, ctx[b].T], axis=1))
        v = vec_common.copy()
        cb = _vecT(c[b])
        v[:, V_CT:V_CT + 16] = np.stack([cb, ccT], axis=-1).reshape(128, 16)
        m["vecs"] = v
        in_maps.append(m)
    return in_maps


_CACHE = {}


def kernel(**inputs):
    in_maps = prepare_inputs(**inputs)
    if "nc" not in _CACHE:
        _CACHE["nc"] = build_program()[0]
    nc = _CACHE["nc"]
    res = run_bass_kernel_spmd(nc, in_maps, core_ids=list(range(NCORES)))
    out = np.stack([np.ascontiguousarray(r["yT"].T) for r in res.results], axis=0)
    return out.astype(np.float32)
```
